# Optimizing a Trainium2 kernel written in Bass

```python
import math
import jax, jax.numpy as jnp
from jax import lax
import numpy as np

D_MODEL = 1024
BATCH = 1
SEQ = 16384
DEPTH = 4

D_FF = 2816
HEAD_DIM = 64
N_Q_HEADS = 16
N_KV_HEADS = 2
Q_PER_KV = N_Q_HEADS // N_KV_HEADS
ATTN_WIDTH = N_Q_HEADS * HEAD_DIM
KV_WIDTH = N_KV_HEADS * HEAD_DIM
WINDOW = 128
ATTN_BLOCK = 128
SGU_CHUNK = 128
SGU_GROUPS = 8
SGU_GROUP_CH = 128
SGU_WIDTH = SGU_GROUPS * SGU_GROUP_CH
IN_SPLIT_SIZES = (ATTN_WIDTH, KV_WIDTH, KV_WIDTH, SGU_WIDTH, SGU_WIDTH, D_MODEL, D_MODEL)
IN_WIDTH = sum(IN_SPLIT_SIZES)
IN_SPLIT_POINTS = tuple(int(p) for p in np.cumsum(IN_SPLIT_SIZES)[:-1])

RMS_EPS = 1e-6
LN_EPS = 1e-5
MASK_VALUE = -1e30

kernel_name = "hybrid_swa_sink_gmlp_macaron_sandwich"


def rms_norm(x, g):
    x32 = x.astype(jnp.float32)
    y = x32 * lax.rsqrt(jnp.mean(x32 * x32, axis=-1, keepdims=True) + RMS_EPS)
    return (y * g.astype(jnp.float32)).astype(x.dtype)


def layer_norm(x, g, b):
    x32 = x.astype(jnp.float32)
    mu = jnp.mean(x32, axis=-1, keepdims=True)
    xc = x32 - mu
    y = xc * lax.rsqrt(jnp.mean(xc * xc, axis=-1, keepdims=True) + LN_EPS)
    return (y * g.astype(jnp.float32) + b.astype(jnp.float32)).astype(x.dtype)


def swiglu(x, w1, w2):
    g, u = jnp.split(x @ w1, 2, axis=-1)
    return (jax.nn.silu(g) * u) @ w2


def sliding_window_attention(q, k, v, sinks):
    B, S, _ = q.shape
    nb = S // ATTN_BLOCK
    qb = q.reshape(B, nb, ATTN_BLOCK, N_KV_HEADS, Q_PER_KV, HEAD_DIM).astype(jnp.float32)
    kb = k.reshape(B, nb, ATTN_BLOCK, N_KV_HEADS, HEAD_DIM)
    vb = v.reshape(B, nb, ATTN_BLOCK, N_KV_HEADS, HEAD_DIM)
    kpad = jnp.zeros_like(kb[:, :1])
    vpad = jnp.zeros_like(vb[:, :1])
    k2 = jnp.concatenate([jnp.concatenate([kpad, kb[:, :-1]], axis=1), kb], axis=2).astype(jnp.float32)
    v2 = jnp.concatenate([jnp.concatenate([vpad, vb[:, :-1]], axis=1), vb], axis=2).astype(jnp.float32)
    scale = 1.0 / math.sqrt(HEAD_DIM)
    scores = jnp.einsum('bnqgrd,bnkgd->bngrqk', qb, k2) * scale
    qi = jnp.arange(ATTN_BLOCK)[:, None]
    kj = jnp.arange(2 * ATTN_BLOCK)[None, :]
    rel = qi + ATTN_BLOCK - kj
    band = (rel >= 0) & (rel < WINDOW)
    blk = jnp.arange(nb)[:, None, None]
    valid = band[None] & ((blk > 0) | (kj[None] >= ATTN_BLOCK))
    scores = jnp.where(valid[None, :, None, None], scores, MASK_VALUE)
    sink = sinks.astype(jnp.float32).reshape(1, 1, N_KV_HEADS, Q_PER_KV, 1)
    m = jnp.maximum(scores.max(axis=-1), sink)
    p = jnp.exp(scores - m[..., None])
    probs = p / (p.sum(axis=-1) + jnp.exp(sink - m))[..., None]
    out = jnp.einsum('bngrqk,bnkgd->bnqgrd', probs, v2)
    return out.reshape(B, S, ATTN_WIDTH).astype(q.dtype)


def spatial_gating(u, v, ln_g, ln_b, w_s, b_s):
    B, S, _ = u.shape
    nc = S // SGU_CHUNK
    vn = layer_norm(v, ln_g, ln_b).reshape(B, nc, SGU_CHUNK, SGU_GROUPS, SGU_GROUP_CH)
    causal = jnp.tril(jnp.ones((SGU_CHUNK, SGU_CHUNK), dtype=bool))
    w = jnp.where(causal[None], w_s, jnp.zeros_like(w_s))
    s = jnp.einsum('gts,bnsgc->bntgc', w, vn) + b_s.T[None, None, :, :, None]
    return u * s.reshape(B, S, SGU_WIDTH)


def setup_inputs(seed: int = 0) -> dict:
    key = jax.random.key(seed)
    ks = iter(jax.random.split(key, 32))
    f32 = jnp.float32

    def nrm(shape, fan_in, scale=1.0):
        return jax.random.normal(next(ks), shape, f32) * (scale * fan_in ** -0.5)

    def gain(shape):
        return 1.0 + 0.05 * jax.random.normal(next(ks), shape, f32)

    L, D = DEPTH, D_MODEL
    return {
        "x": jax.random.normal(next(ks), (BATCH, SEQ, D), f32),
        "ffn1_pre_g": gain((L, D)),
        "ffn1_w1": nrm((L, D, 2 * D_FF), D),
        "ffn1_w2": nrm((L, D_FF, D), D_FF),
        "ffn1_post_g": gain((L, D)),
        "mix_pre_g": gain((L, D)),
        "w_in": nrm((L, D, IN_WIDTH), D),
        "attn_sinks": 0.5 * jax.random.normal(next(ks), (L, N_Q_HEADS), f32),
        "sgu_ln_g": gain((L, SGU_WIDTH)),
        "sgu_ln_b": 0.02 * jax.random.normal(next(ks), (L, SGU_WIDTH), f32),
        "sgu_w": nrm((L, SGU_GROUPS, SGU_CHUNK, SGU_CHUNK), SGU_CHUNK, 0.5),
        "sgu_b": gain((L, SGU_GROUPS, SGU_CHUNK)),
        "w_attn_branch": nrm((L, ATTN_WIDTH, D), ATTN_WIDTH),
        "w_sgu_branch": nrm((L, SGU_WIDTH, D), SGU_WIDTH),
        "w_out": nrm((L, D, D), D),
        "mix_post_g": gain((L, D)),
        "ffn2_pre_g": gain((L, D)),
        "ffn2_w1": nrm((L, D, 2 * D_FF), D),
        "ffn2_w2": nrm((L, D_FF, D), D_FF),
        "ffn2_post_g": gain((L, D)),
    }


def reference(x, ffn1_pre_g, ffn1_w1, ffn1_w2, ffn1_post_g, mix_pre_g, w_in, attn_sinks,
              sgu_ln_g, sgu_ln_b, sgu_w, sgu_b, w_attn_branch, w_sgu_branch, w_out, mix_post_g,
              ffn2_pre_g, ffn2_w1, ffn2_w2, ffn2_post_g):
    for l in range(DEPTH):
        h = rms_norm(x, ffn1_pre_g[l])
        x = x + 0.5 * rms_norm(swiglu(h, ffn1_w1[l], ffn1_w2[l]), ffn1_post_g[l])

        h = rms_norm(x, mix_pre_g[l])
        z = h @ w_in[l]
        q, k, v, u_s, v_s, g_a, g_b = jnp.split(z, IN_SPLIT_POINTS, axis=-1)
        y_attn = sliding_window_attention(q, k, v, attn_sinks[l])
        y_sgu = spatial_gating(jax.nn.gelu(u_s, approximate=False), jax.nn.gelu(v_s, approximate=False),
                               sgu_ln_g[l], sgu_ln_b[l], sgu_w[l], sgu_b[l])
        merged = (jax.nn.sigmoid(g_a) * (y_attn @ w_attn_branch[l])
                  + jax.nn.sigmoid(g_b) * (y_sgu @ w_sgu_branch[l]))
        x = x + rms_norm(merged @ w_out[l], mix_post_g[l])

        h = rms_norm(x, ffn2_pre_g[l])
        x = x + 0.5 * rms_norm(swiglu(h, ffn2_w1[l], ffn2_w2[l]), ffn2_post_g[l])
    return x
```

```python
import contextlib
import numpy as np
import concourse.bass as bass
import concourse.mybir as mybir
from concourse.bass_utils import run_bass_kernel_spmd

F32 = mybir.dt.float32
BF16 = mybir.dt.bfloat16
AF = mybir.ActivationFunctionType
ALU = mybir.AluOpType

D = 1024
KC = 8
DFF = 2816
JC = 22
INW = 5376
NEG = -30000.0
RMS_EPS = 1e-6
LN_EPS = 1e-5
SAME_ENG_SYNC = True
RING = 4
SLOT_ELEMS = JC * 128


class Sem:
    def __init__(self, h):
        self.h = h
        self.n = 0


class Eng:
    def __init__(self, name, sem, in_order=False):
        self.name = name
        self.sem = sem
        self.q = []
        self.seen = {}
        self.in_order = in_order


class Reg:
    __slots__ = ("w", "r", "deps")

    def __init__(self, deps=None):
        self.w = None
        self.r = {}
        self.deps = dict(deps) if deps else {}


class Area:
    prog = None
    dirty = False

    def __init__(self):
        self.regs = []
        self.deps = {}

    def new(self):
        if self.dirty:
            if self.prog is not None:
                self.prog.flush_all()
            d = dict(self.deps)
            for r in self.regs:
                for st in ([r.w] if r.w else []) + list(r.r.values()):
                    k = id(st[0])
                    if k not in d or d[k][1] < st[1]:
                        d[k] = st
            self.deps = d
            self.regs = []
            self.dirty = False
        r = Reg(self.deps)
        self.regs.append(r)
        return r

    def close(self):
        self.dirty = True


class _Rec:
    def __getattr__(self, name):
        def f(*a, **k):
            self.call = (name, a, k)
            return self
        return f


class Prog:
    def __init__(self):
        self.dry = False
        self.engs = {}
        self.deferred = []
        self.pending = {}

    def defer(self, fn, regs, after):
        if self.dry:
            fn()
            return
        item = [after, fn, set(id(r) for r in regs)]
        self.deferred.append(item)
        for k in item[2]:
            self.pending[k] = item

    def _run(self, item):
        self.deferred.remove(item)
        for k in item[2]:
            if self.pending.get(k) is item:
                del self.pending[k]
        item[1]()

    def _run_upto(self, item):
        self._run(item)

    def tick(self, n):
        if self.dry:
            return
        for it in self.deferred:
            it[0] -= n
        while True:
            ready = [it for it in self.deferred if it[0] <= 0]
            if not ready:
                break
            self._run(ready[0])

    def flush_all(self):
        while self.deferred:
            self._run(self.deferred[0])

    def emit(self, eng, fn, reads=(), writes=(), inc=True, dma_sem=None, stamp_only=()):
        if self.dry:
            return None
        while self.pending:
            hit = None
            for rg in list(reads) + list(writes):
                hit = self.pending.get(id(rg))
                if hit is not None:
                    break
            if hit is None:
                break
            self._run(hit)
        deps = {}

        def add(st):
            if st is None:
                return
            k = id(st[0])
            if k not in deps or deps[k][1] < st[1]:
                deps[k] = st

        for rg in reads:
            add(rg.w)
            for st in rg.deps.values():
                add(st)
        for rg in writes:
            add(rg.w)
            for st in rg.r.values():
                add(st)
            for st in rg.deps.values():
                add(st)
        waits = []
        for k, (s, v) in deps.items():
            if s is eng.sem and (eng.in_order or not SAME_ENG_SYNC):
                continue
            if eng.seen.get(k, 0) < v:
                eng.seen[k] = v
                waits.append((s.h, v))
        if dma_sem is not None:
            dma_sem.n += 16
            stamp = (dma_sem, dma_sem.n)
            kind = 2
        elif inc:
            eng.sem.n += 1
            stamp = (eng.sem, eng.sem.n)
            kind = 1
        else:
            stamp = (eng.sem, eng.sem.n + 1)
            kind = 0
        semh = stamp[0].h
        rec = _Rec()
        fn(rec)
        cname, cargs, ckw = rec.call

        def run(e):
            for (h, v) in waits:
                e.wait_ge(h, v)
            ins = getattr(e, cname)(*cargs, **ckw)
            if kind == 2:
                ins.then_inc(semh, 16)
            elif kind == 1:
                ins.then_inc(semh, 1)

        eng.q.append(run)
        for rg in reads:
            k = id(stamp[0])
            if k not in rg.r or rg.r[k][1] < stamp[1]:
                rg.r[k] = stamp
        for rg in list(writes) + list(stamp_only):
            rg.w = stamp
            rg.r = {}
            rg.deps = {}
        return stamp


def tiles_of(a0, a1):
    out = []
    e = a1
    while e > a0:
        n = min(4, e - a0)
        out.append((e - n, n))
        e -= n
    return out[::-1]


def build(cfg):
    NL = cfg["layers"]
    HALO = cfg["halo"]
    OWN = cfg["own"]
    NS = HALO + OWN
    GROUPS = cfg["groups"]
    GS = max(b - a for a, b in GROUPS)
    T = GS * 128

    nc = bass.Bass("TRN2", target_bir_lowering=False)
    dt = nc.dram_tensor
    xT_d = dt("xT", [D, NS * 128], F32, kind="ExternalInput").ap()
    w1_d = [dt("ffn1_w1", [NL, D, 2 * DFF], F32, kind="ExternalInput").ap(),
            dt("ffn2_w1", [NL, D, 2 * DFF], F32, kind="ExternalInput").ap()]
    w2_d = [dt("ffn1_w2", [NL, DFF, D], F32, kind="ExternalInput").ap(),
            dt("ffn2_w2", [NL, DFF, D], F32, kind="ExternalInput").ap()]
    win_d = dt("w_in", [NL, D, INW], F32, kind="ExternalInput").ap()
    wa_d = dt("w_attn_branch", [NL, D, D], F32, kind="ExternalInput").ap()
    ws_d = dt("w_sgu_branch", [NL, D, D], F32, kind="ExternalInput").ap()
    wo_d = dt("w_out", [NL, D, D], F32, kind="ExternalInput").ap()
    par_d = dt("params", [128, NL * 8 * 8], F32, kind="ExternalInput").ap()
    sink_d = dt("sinks_r", [NL, 2, 8], F32, kind="ExternalInput").ap()
    sguw_d = dt("sgu_wT", [NL, 128, 1024], F32, kind="ExternalInput").ap()
    sgub_d = dt("sgu_b", [NL, 1024], F32, kind="ExternalInput").ap()
    cst_d = dt("consts", [128, 128 + 3 * 128 + 128], F32, kind="ExternalInput").ap()
    out_d = dt("outT", [D, OWN * 128], F32, kind="ExternalOutput").ap()

    es = contextlib.ExitStack()
    P = Prog()

    def sb(name, shape, dtype):
        return es.enter_context(nc.sbuf_tensor(name, shape, dtype))

    def sem(name):
        return Sem(es.enter_context(nc.semaphore(name)))

    with es:
        xT = sb("xT_sb", [128, KC, T], F32)
        hT = sb("hT_sb", [128, KC, T], BF16)
        big = sb("big_sb", [128, JC * T], BF16)
        ybuf = sb("y_sb", [128, KC * T], F32)
        ring = [sb(f"ring{i}", [128, SLOT_ELEMS], BF16) for i in range(RING)]
        ones_bf = sb("ones_bf", [128, 128], BF16)
        ident_bf = sb("ident_bf", [128, 128], BF16)
        masks_bf = sb("masks_bf", [128, 3, 128], BF16)
        tri_bf = sb("tri_bf", [128, 128], BF16)
        eps_t = sb("eps_t", [128, 2], F32)
        par = sb("par_sb", [128, NL, 8, 8], F32)
        parh = sb("parh_sb", [128, NL, 3, 8], F32)
        wtm = sb("wtm_sb", [128, 8, 128], BF16)
        bt = sb("bt_sb", [128, 8, 128], F32)
        esr = sb("esr_sb", [128, 8], F32)
        ese = sb("ese_sb", [128, 8], F32)
        bk = sb("bk_sb", [128, NL, 2, 128], BF16)
        bv = sb("bv_sb", [128, NL, 128], BF16)
        wk = sb("wk_sb", [128, 6, 512], F32)
        hk = sb("hk_sb", [128, 6, 512], BF16)
        st = sb("st_sb", [128, 16], F32)
        psum = [es.enter_context(nc.psum_tensor(f"ps{i}", [128, 512], F32)) for i in range(8)]

        pe = Eng("pe", sem("s_pe"), in_order=True)
        act = Eng("act", sem("s_act"))
        dve = Eng("dve", sem("s_dve"))
        pool = Eng("pool", sem("s_pool"))
        sp = Eng("sp", sem("s_sp"))
        s_ring = [sem(f"s_ring{i}") for i in range(RING)]
        s_x = sem("s_x")
        s_o = sem("s_o")
        s_c = sem("s_c")
        s_par = sem("s_par")
        s_wraw = sem("s_wraw")
        s_bt = sem("s_bt")
        s_esr = sem("s_esr")

        r_ps = [Reg() for _ in range(8)]
        r_ring = [Reg() for _ in range(RING)]
        r_wk = [Reg() for _ in range(6)]
        r_hk = [Reg() for _ in range(6)]
        r_const = Reg()
        r_par = Reg()
        r_parh = Reg()
        r_x = {}
        r_h = {}
        for k in range(KC):
            for s_ in range(GS):
                r_x[(k, s_)] = Reg()
                r_h[(k, s_)] = Reg()
        a_big = Area()
        a_y = Area()
        a_big.prog = None
        a_y.prog = P
        r_wraw, r_wtm, r_bt, r_esr, r_ese = Reg(), Reg(), Reg(), Reg(), Reg()
        r_bk = [Reg() for _ in range(NL)]
        r_st = Reg()

        def xr(k, s0, n):
            return [r_x[(k, s)] for s in range(s0, s0 + n)]

        def hr(k, s0, n):
            return [r_h[(k, s)] for s in range(s0, s0 + n)]

        pieces = []
        state = {"use": 0, "issued": 0}

        def piece(subs, hold_prev=False, live=1):
            idx = state["use"]
            state["use"] += 1
            if P.dry:
                pieces.append(subs)
                return ring[idx % RING], r_ring[idx % RING]
            while state["issued"] < min(len(pieces), idx - (live - 1) + RING):
                i = state["issued"]
                sl = i % RING
                for si, (dst_fn, src) in enumerate(pieces[i]):
                    dst = dst_fn(ring[sl])
                    P.emit(pool, lambda e, dst=dst, src=src: e.dma_start(out=dst, in_=src),
                           writes=[r_ring[sl]] if si == 0 else (), stamp_only=() if si == 0 else [r_ring[sl]],
                           dma_sem=s_ring[sl])
                state["issued"] += 1
            return ring[idx % RING], r_ring[idx % RING]

        def wview(w2d, c0, ncols):
            return w2d.rearrange("(k p) c -> p k c", p=128)[:, :, c0:c0 + ncols]

        def slotv(width, nk, c0=0, nc_=None):
            nc_ = width if nc_ is None else nc_
            return lambda rt: rt[:, 0:nk * width].rearrange("p (k c) -> p k c", c=width)[:, :, c0:c0 + nc_]

        def mm_group(bank, out_ap, terms, extra_reads=()):
            n = len(terms)
            allregs = list(extra_reads)
            for t_ in terms:
                allregs += t_[2]
            for i, (l_, r_, tregs) in enumerate(terms):
                P.emit(pe, lambda e, l_=l_, r_=r_, i=i: e.matmul(out_ap, l_, r_, start=(i == 0), stop=(i == n - 1)),
                       reads=(list(extra_reads) + list(tregs)) if i == 0 else tregs, writes=[r_ps[bank]], inc=(i == n - 1))
            fin = (pe.sem, pe.sem.n)
            if not P.dry:
                for rg in allregs:
                    rg.r[id(pe.sem)] = fin
            P.tick(n)

        def A(fn, reads, writes):
            P.emit(act, fn, reads, writes)

        def V(fn, reads, writes):
            P.emit(dve, fn, reads, writes)

        def ss_to_rstd(bank, n, eps_col, wtmp, wout):
            A(lambda e: e.activation(out=wk[:, wtmp, 0:n], in_=psum[bank][:, 0:n], func=AF.Ln,
                                     bias=eps_t[:, eps_col:eps_col + 1], scale=1.0 / D),
              [r_ps[bank], r_const], [r_wk[wtmp]])
            A(lambda e: e.activation(out=wk[:, wout, 0:n], in_=wk[:, wtmp, 0:n], func=AF.Exp, scale=-0.5),
              [r_wk[wtmp]], [r_wk[wout]])

        cnt = {"sq": 0}

        def prenorm_tile_a(s0, ns):
            n = ns * 128
            t0 = s0 * 128
            for k in range(KC):
                A(lambda e: e.activation(out=hT[:, k, t0:t0 + n], in_=xT[:, k, t0:t0 + n], func=AF.Square),
                  xr(k, s0, ns), hr(k, s0, ns))

        def prenorm_tile_b(l, gi_par, s0, ns):
            n = ns * 128
            t0 = s0 * 128
            for k in range(KC):
                P.emit(pe, lambda e: e.matmul(psum[7][:, 0:n], ones_bf[:, :], hT[:, k, t0:t0 + n],
                                              start=(k == 0), stop=(k == KC - 1)),
                       reads=hr(k, s0, ns) + [r_const], writes=[r_ps[7]], inc=True)
            ss_to_rstd(7, n, 0, 2, 3)
            for k in range(KC):
                V(lambda e: e.scalar_tensor_tensor(out=hT[:, k, t0:t0 + n], in0=xT[:, k, t0:t0 + n],
                                                   scalar=par[:, l, gi_par, k:k + 1], in1=wk[:, 3, 0:n],
                                                   op0=ALU.mult, op1=ALU.mult),
                  xr(k, s0, ns) + [r_wk[3], r_par], hr(k, s0, ns))

        def prenorm_tile(l, gi_par, s0, ns):
            prenorm_tile_a(s0, ns)
            prenorm_tile_b(l, gi_par, s0, ns)

        def prenorm(l, gi_par, tl):
            for (s0, ns) in tl:
                prenorm_tile(l, gi_par, s0, ns)

        def out_and_postnorm(l, gi_parh, tl, nk, src_fn, src_regs_fn, mk_piece, after_tile=None, mp=1):
            yv = ybuf[:, :].rearrange("p (k t) -> p k t", t=T)
            yreg = {}
            pend_ss = []
            npc = KC // mp
            ntail = min(RING - 1, npc)
            order = []
            for pc in range(npc - ntail):
                for mi_ in range(mp):
                    for ti in range(len(tl)):
                        order.append((pc, pc * mp + mi_, ti))
            for ti in range(len(tl)):
                for pc in range(npc - ntail, npc):
                    for mi_ in range(mp):
                        order.append((pc, pc * mp + mi_, ti))
            loaded = {}
            gcount = 0
            for (pc, m, ti) in order:
                if pc not in loaded:
                    if pc >= npc - ntail:
                        for p2 in range(npc - ntail, npc):
                            loaded[p2] = mk_piece(p2, p2 - (npc - ntail) + 1)
                    else:
                        loaded[pc] = mk_piece(pc, 1)
                rt, rr = loaded[pc]
                wv = rt[:, 0:nk * mp * 128].rearrange("p (k c) -> p k c", c=mp * 128)
                mo = (m % mp) * 128
                s0, ns = tl[ti]
                n = ns * 128
                t0 = s0 * 128
                bank = gcount % 2
                gcount += 1
                last = (m == KC - 1)
                first = (m == 0)
                mm_group(bank, psum[bank][:, 0:n],
                         [(wv[:, k, mo:mo + 128], src_fn(k, t0, n), src_regs_fn(k, s0, ns)) for k in range(nk)],
                         extra_reads=[rr])
                yr = a_y.new()
                yreg[(m, ti)] = yr
                hb = cnt["sq"] % 2
                cnt["sq"] += 1
                A(lambda e: e.activation(out=hk[:, hb, 0:n], in_=psum[bank][:, 0:n], func=AF.Square),
                  [r_ps[bank]], [r_hk[hb]])
                A(lambda e: e.activation(out=yv[:, m, t0:t0 + n], in_=psum[bank][:, 0:n], func=AF.Copy,
                                         scale=parh[:, l, gi_parh, m:m + 1]),
                  [r_ps[bank], r_parh], [yr])

                def ss_mm(hb=hb, ti=ti, n=n, first=first, last=last):
                    P.emit(pe, lambda e: e.matmul(psum[4 + ti][:, 0:n], ones_bf[:, :], hk[:, hb, 0:n],
                                                  start=first, stop=last),
                           reads=[r_hk[hb], r_const], writes=[r_ps[4 + ti]], inc=True)
                if pend_ss:
                    pend_ss.pop()()
                if not last:
                    pend_ss.append(ss_mm)
                else:
                    def chain_a(ss_mm=ss_mm, ti=ti, s0=s0, ns=ns, n=n, t0=t0):
                        ss_mm()
                        ss_to_rstd(4 + ti, n, 0, 2, 3)
                        def dmul(mm_):
                            wb = 4 + (mm_ % 2)
                            V(lambda e: e.tensor_tensor(out=wk[:, wb, 0:n], in0=yv[:, mm_, t0:t0 + n], in1=wk[:, 3, 0:n], op=ALU.mult),
                              [yreg[(mm_, ti)], r_wk[3]], [r_wk[wb]])

                        def dadd(mm_):
                            wb = 4 + (mm_ % 2)
                            V(lambda e: e.tensor_tensor(out=xT[:, mm_, t0:t0 + n], in0=xT[:, mm_, t0:t0 + n],
                                                        in1=wk[:, wb, 0:n], op=ALU.add),
                              [r_wk[wb]] + xr(mm_, s0, ns), xr(mm_, s0, ns))
                        dmul(0)
                        for mm_ in range(1, KC):
                            dmul(mm_)
                            dadd(mm_ - 1)
                        dadd(KC - 1)
                        if after_tile is not None:
                            after_tile(s0, ns)

                    regs = [r_ps[4 + ti]]
                    for k in range(KC):
                        regs += xr(k, s0, ns) + hr(k, s0, ns)
                    P.defer(chain_a, regs, 8)
            assert not pend_ss
            a_y.close()

        def ffn(l, which, tl, after_tile=None):
            w1 = w1_d[which][l]
            w2 = w2_d[which][l]
            hid = big[:, :].rearrange("p (j t) -> p j t", t=T)
            hreg = {}
            nhead = (RING - 1) if len(tl) > 1 else 0
            order = [(j, ti) for ti in range(len(tl)) for j in range(nhead)]
            order += [(j, ti) for j in range(nhead, JC) for ti in range(len(tl))]
            loaded = {}

            def w1_piece(j, live):
                return piece([(slotv(256, KC, 0, 128), wview(w1, j * 128, 128)),
                              (slotv(256, KC, 128, 128), wview(w1, DFF + j * 128, 128))], live=live)

            gcount = 0
            for (j, ti) in order:
                if j not in loaded:
                    if j < nhead:
                        for j2 in range(nhead):
                            loaded[j2] = w1_piece(j2, j2 + 1)
                    else:
                        loaded[j] = w1_piece(j, 1)
                rt, rr = loaded[j]
                wv = rt[:, 0:KC * 256].rearrange("p (k c) -> p k c", c=256)
                s0, ns = tl[ti]
                n = ns * 128
                t0 = s0 * 128
                par2 = gcount % 2
                gcount += 1
                bg, bu = par2, 2 + par2
                mm_group(bg, psum[bg][:, 0:n], [(wv[:, k, 0:128], hT[:, k, t0:t0 + n], hr(k, s0, ns)) for k in range(KC)], [rr])
                mm_group(bu, psum[bu][:, 0:n], [(wv[:, k, 128:256], hT[:, k, t0:t0 + n], hr(k, s0, ns)) for k in range(KC)], [rr])
                A(lambda e: e.activation(out=wk[:, par2, 0:n], in_=psum[bg][:, 0:n], func=AF.Silu),
                  [r_ps[bg]], [r_wk[par2]])
                rg = a_big.new()
                hreg[(j, ti)] = rg
                V(lambda e: e.tensor_tensor(out=hid[:, j, t0:t0 + n], in0=wk[:, par2, 0:n],
                                            in1=psum[bu][:, 0:n], op=ALU.mult),
                  [r_wk[par2], r_ps[bu]], [rg])
            timap = {s0: ti for ti, (s0, ns) in enumerate(tl)}

            def mk_piece(m, live):
                return piece([(slotv(128, JC), wview(w2, m * 128, 128))], live=live)

            out_and_postnorm(l, 0 if which == 0 else 2, tl, JC,
                             lambda j, t0, n: hid[:, j, t0:t0 + n],
                             lambda j, s0, ns: [hreg[(j, timap[s0])]], mk_piece, after_tile)
            a_big.close()

        def mixer(l, gidx, a0, f0, a1, tl, tlf, first_own_local, after_tile=None):
            win = win_d[l]
            o = 0
            qT = big[:, o:o + KC * T].rearrange("p (k t) -> p k t", t=T); o += KC * T
            uT = big[:, o:o + KC * T].rearrange("p (k t) -> p k t", t=T); o += KC * T
            kT = big[:, o:o + 2 * (T + 128)].rearrange("p (g t) -> p g t", t=T + 128); o += 2 * (T + 128)
            vv = big[:, o:o + (GS + 1) * 128].rearrange("p (s c) -> p s c", c=128); o += (GS + 1) * 128
            assert o <= JC * T
            vg = ybuf[:, :].bitcast(BF16)[:, 0:GS * 1024].rearrange("p (s c) -> p s c", c=1024)
            gates = ybuf[:, :].bitcast(BF16)[:, 0:16 * T].rearrange("p (k t) -> p k t", t=T)
            r_q = {(g, s): a_big.new() for g in range(2) for s in range(f0, a1)}
            r_u = {(c, s): a_big.new() for c in range(KC) for s in range(f0, a1)}
            r_k = {s: a_big.new() for s in range(a0 - 1, a1)}
            r_v = {s: a_big.new() for s in range(a0 - 1, a1)}
            P.emit(pool, lambda e: e.dma_start(out=wtm[:, :, :].rearrange("p g t -> p (g t)"), in_=sguw_d[l]), writes=[r_wtm], dma_sem=s_wraw)
            for g in range(8):
                V(lambda e, g=g: e.tensor_tensor(out=wtm[:, g, :], in0=wtm[:, g, :], in1=tri_bf[:, :], op=ALU.mult),
                  [r_wtm, r_const], [r_wtm])
            P.emit(sp, lambda e: e.dma_start(out=bt[:, :, :].rearrange("p g t -> p (g t)"),
                                             in_=sgub_d[l].partition_broadcast(128)), writes=[r_bt], dma_sem=s_bt)
            for e_ in range(2):
                P.emit(sp, lambda e, e_=e_: e.dma_start(out=esr[64 * e_:64 * e_ + 64, :],
                                                        in_=sink_d[l, e_].partition_broadcast(64)),
                       writes=[r_esr] if e_ == 0 else (), stamp_only=() if e_ == 0 else [r_esr], dma_sem=s_esr)
            A(lambda e: e.activation(out=ese[:, :], in_=esr[:, :], func=AF.Exp), [r_esr], [r_ese])
            wtm2 = wtm[:, :, :].rearrange("p g t -> p (g t)")
            for hf in range(2):
                P.emit(pe, lambda e, hf=hf: e.matmul(psum[6 + hf][:, :], ones_bf[:, :], wtm2[:, hf * 512:(hf + 1) * 512],
                                                     start=True, stop=True),
                       reads=[r_wtm, r_const], writes=[r_ps[6 + hf]], inc=True)
                for gg in range(4):
                    g = hf * 4 + gg
                    V(lambda e, g=g, gg=gg, hf=hf: e.scalar_tensor_tensor(out=bt[:, g, :], in0=psum[6 + hf][:, gg * 128:(gg + 1) * 128],
                                                                          scalar=par[:, l, 7, g:g + 1], in1=bt[:, g, :],
                                                                          op0=ALU.mult, op1=ALU.add),
                      [r_ps[6 + hf], r_par, r_bt], [r_bt])
            if gidx == 0:
                V(lambda e: e.memset(kT[:, :, 0:128], 0.0), [], [r_k[a0 - 1]])
                V(lambda e: e.memset(vv[:, 0, :], 0.0), [], [r_v[a0 - 1]])
            else:
                V(lambda e: e.tensor_copy(out=kT[:, :, 0:128], in_=bk[:, l, :, :]), [r_bk[l]], [r_k[a0 - 1]])
                V(lambda e: e.tensor_copy(out=vv[:, 0, :], in_=bv[:, l, :]), [r_bk[l]], [r_v[a0 - 1]])
            KOFF = 1024
            rtk, rrk = piece([(slotv(256, KC, 0, 64), wview(win, KOFF, 64)), (slotv(256, KC, 64, 64), wview(win, KOFF, 64)),
                              (slotv(256, KC, 128, 64), wview(win, KOFF + 64, 64)), (slotv(256, KC, 192, 64), wview(win, KOFF + 64, 64))])
            wvk = rtk[:, 0:KC * 256].rearrange("p (k c) -> p k c", c=256)
            rtv, rrv = piece([(slotv(128, KC), wview(win, KOFF + 128, 128))], live=2)
            wvv = rtv[:, 0:KC * 128].rearrange("p (k c) -> p k c", c=128)
            rtq0, rrq0 = piece([(slotv(256, KC), wview(win, 0, 256))], live=3)
            wvq0 = rtq0[:, 0:KC * 256].rearrange("p (k c) -> p k c", c=256)
            tlf_set = set(tlf)
            bi = 0
            for (s0, ns) in tl:
                n, t0 = ns * 128, s0 * 128
                for g in range(2):
                    bank = bi % 4; bi += 1
                    mm_group(bank, psum[bank][:, 0:n], [(wvk[:, k, g * 128:(g + 1) * 128], hT[:, k, t0:t0 + n], hr(k, s0, ns)) for k in range(KC)], [rrk])
                    lp = (s0 - a0 + 1) * 128
                    A(lambda e: e.activation(out=kT[:, g, lp:lp + n], in_=psum[bank][:, 0:n], func=AF.Copy),
                      [r_ps[bank]], [r_k[s_] for s_ in range(s0, s0 + ns)])
                for s_ in range(s0, s0 + ns):
                    bank = bi % 4; bi += 1
                    ts_ = s_ * 128
                    mm_group(bank, psum[bank][:, 0:128], [(hT[:, k, ts_:ts_ + 128], wvv[:, k, :], hr(k, s_, 1)) for k in range(KC)], [rrv])
                    A(lambda e: e.activation(out=vv[:, s_ - a0 + 1, :], in_=psum[bank][:, 0:128], func=AF.Copy),
                      [r_ps[bank]], [r_v[s_]])
                qs0 = max(s0, f0)
                qns = s0 + ns - qs0
                if qns > 0:
                    assert (qs0, qns) in tlf_set, ((qs0, qns), tlf)
                    qn, qt0 = qns * 128, qs0 * 128
                    for c in range(2):
                        bank = bi % 4; bi += 1
                        mm_group(bank, psum[bank][:, 0:qn], [(wvq0[:, k, c * 128:(c + 1) * 128], hT[:, k, qt0:qt0 + qn], hr(k, qs0, qns)) for k in range(KC)], [rrq0])
                        A(lambda e: e.activation(out=qT[:, c, qt0:qt0 + qn], in_=psum[bank][:, 0:qn], func=AF.Copy),
                          [r_ps[bank]], [r_q[(0, s_)] for s_ in range(qs0, qs0 + qns)])
            if gidx + 1 < len(GROUPS):
                lpl = (a1 - 1 - a0 + 1) * 128
                V(lambda e: e.tensor_copy(out=bk[:, l, :, :], in_=kT[:, :, lpl:lpl + 128]), [r_k[a1 - 1]], [r_bk[l]])
                V(lambda e: e.tensor_copy(out=bv[:, l, :], in_=vv[:, a1 - 1 - a0 + 1, :]), [r_v[a1 - 1]], [r_bk[l]])
            for pc in range(1, 4):
                rt, rr = piece([(slotv(256, KC), wview(win, pc * 256, 256))])
                wv = rt[:, 0:KC * 256].rearrange("p (k c) -> p k c", c=256)
                for cc in range(2):
                    c = pc * 2 + cc
                    for (s0, ns) in tlf:
                        n, t0 = ns * 128, s0 * 128
                        bank = bi % 4; bi += 1
                        mm_group(bank, psum[bank][:, 0:n], [(wv[:, k, cc * 128:(cc + 1) * 128], hT[:, k, t0:t0 + n], hr(k, s0, ns)) for k in range(KC)], [rr])
                        A(lambda e, c=c, bank=bank, t0=t0, n=n: e.activation(out=qT[:, c, t0:t0 + n], in_=psum[bank][:, 0:n], func=AF.Copy),
                          [r_ps[bank]], [r_q[(c // 4, s)] for s in range(s0, s0 + ns)])
            UOFF = 1280
            for pc in range(4):
                rt, rr = piece([(slotv(256, KC), wview(win, UOFF + pc * 256, 256))])
                wv = rt[:, 0:KC * 256].rearrange("p (k c) -> p k c", c=256)
                for cc in range(2):
                    c = pc * 2 + cc
                    for (s0, ns) in tlf:
                        n, t0 = ns * 128, s0 * 128
                        bank = bi % 4; bi += 1
                        mm_group(bank, psum[bank][:, 0:n], [(wv[:, k, cc * 128:(cc + 1) * 128], hT[:, k, t0:t0 + n], hr(k, s0, ns)) for k in range(KC)], [rr])
                        A(lambda e, c=c, bank=bank, t0=t0, n=n: e.activation(out=uT[:, c, t0:t0 + n], in_=psum[bank][:, 0:n], func=AF.Gelu),
                          [r_ps[bank]], [r_u[(c, s)] for s in range(s0, s0 + ns)])
            VOFF = 2304
            r_vg = {s: a_y.new() for s in range(f0, a1)}
            for pc in range(4):
                rt, rr = piece([(slotv(256, KC), wview(win, VOFF + pc * 256, 256))])
                wv = rt[:, 0:KC * 256].rearrange("p (k c) -> p k c", c=256)
                for s in range(f0, a1):
                    t0 = s * 128
                    bank = bi % 4; bi += 1
                    mm_group(bank, psum[bank][:, 0:256], [(hT[:, k, t0:t0 + 128], wv[:, k, :], hr(k, s, 1)) for k in range(KC)], [rr])
                    A(lambda e, bank=bank, s=s, pc=pc: e.activation(out=vg[:, s - a0, pc * 256:(pc + 1) * 256], in_=psum[bank][:, 0:256], func=AF.Gelu),
                      [r_ps[bank]], [r_vg[s]])
            pstate = {"pc": 0, "bi": bi}

            def att_scores(s, g, kb):
                t0 = s * 128
                lp = (s - a0 + 1) * 128
                mi = 0 if kb == 1 else (2 if s == first_own_local else 1)
                kl = lp - 128 if kb == 0 else lp
                ks = s - 1 if kb == 0 else s
                banks = []
                for e_ in range(2):
                    bank = pstate["bi"] % 3; pstate["bi"] += 1
                    banks.append(bank)
                    for cc in range(4):
                        P.emit(pe, lambda e: e.matmul(psum[bank][:, cc * 128:(cc + 1) * 128], ident_bf[:, :], masks_bf[:, mi, :],
                                                      start=(cc == 0), stop=False),
                               reads=[r_const, r_k[ks], r_q[(g, s)]] if cc == 0 else (), writes=[r_ps[bank]], inc=False)
                for cc in range(4):
                    c = 4 * g + cc
                    for e_ in range(2):
                        bank = banks[e_]
                        last = (cc == 3 and e_ == 1)
                        P.emit(pe, lambda e: e.matmul(psum[bank][:, cc * 128:(cc + 1) * 128],
                                                      kT[64 * e_:64 * e_ + 64, g, kl:kl + 128],
                                                      qT[64 * e_:64 * e_ + 64, c, t0:t0 + 128], start=False, stop=(cc == 3)),
                               reads=(), writes=[r_ps[bank]], inc=last)
                if not P.dry:
                    fin = (pe.sem, pe.sem.n)
                    r_k[ks].r[id(pe.sem)] = fin
                    r_q[(g, s)].r[id(pe.sem)] = fin
                ptiles = []
                for e_ in range(2):
                    bank = banks[e_]
                    hb = 2 + (pstate["pc"] % 4); pstate["pc"] += 1
                    A(lambda e: e.activation(out=hk[:, hb, :], in_=psum[bank][:, :], func=AF.Exp, scale=0.125),
                      [r_ps[bank]], [r_hk[hb]])
                    ptiles.append(hb)
                return ptiles

            def att_pv(s, g, kb, ptiles):
                bo, bd = 3 + g, 5
                ks = s - 1 if kb == 0 else s
                lpk = (ks - a0 + 1)
                e_ = kb
                for which_, bnk in ((0, bo), (1, bd)):
                    lhs = vv[:, lpk, 64 * g:64 * g + 64] if which_ == 0 else ones_bf[:, 0:64]
                    for ep in range(2):
                        hb = ptiles[ep]
                        last = (which_ == 1 and ep == 1)
                        P.emit(pe, lambda e: e.matmul(psum[bnk][64 * ep:64 * ep + 64, :], lhs, hk[:, hb, :], start=(kb == 0), stop=(kb == 1)),
                               reads=[r_hk[hb], r_v[ks], r_const], writes=[r_ps[bnk]], inc=last)
                if e_ == 1:
                    t0 = s * 128
                    for cc in range(4):
                        A(lambda e: e.activation(out=wk[:, 0, cc * 128:(cc + 1) * 128], in_=psum[bd][:, cc * 128:(cc + 1) * 128], func=AF.Ln,
                                                 bias=ese[:, 4 * g + cc:4 * g + cc + 1], scale=1.0),
                          [r_ps[bd], r_ese], [r_wk[0]])
                    A(lambda e: e.activation(out=wk[:, 1, :], in_=wk[:, 0, :], func=AF.Exp, scale=-1.0), [r_wk[0]], [r_wk[1]])
                    V(lambda e: e.tensor_tensor(out=qT[:, 4 * g:4 * g + 4, t0:t0 + 128],
                                                in0=psum[bo][:, :].rearrange("p (c t) -> p c t", t=128),
                                                in1=wk[:, 1, :].rearrange("p (c t) -> p c t", t=128), op=ALU.mult),
                      [r_ps[bo], r_wk[1]], [r_q[(g, s)]])

            def sgu_stats(s):
                sl = s - a0
                junk = wk[:, 2:4, :].rearrange("p a b -> p (a b)")
                V(lambda e: e.tensor_scalar(out=junk, in0=vg[:, sl, :], scalar1=1.0, scalar2=0.0, op0=ALU.mult, op1=ALU.add, accum_out=st[:, 0:1]),
                  [r_vg[s]], [r_wk[2], r_wk[3], r_st])
                V(lambda e: e.scalar_tensor_tensor(out=junk, in0=vg[:, sl, :], scalar=1.0, in1=vg[:, sl, :], op0=ALU.mult, op1=ALU.mult, accum_out=st[:, 1:2]),
                  [r_vg[s]], [r_wk[2], r_wk[3], r_st])
                V(lambda e: e.tensor_scalar(out=st[:, 2:3], in0=st[:, 0:1], scalar1=1.0 / 1024, scalar2=None, op0=ALU.mult), [r_st], [r_st])
                V(lambda e: e.tensor_tensor(out=st[:, 3:4], in0=st[:, 2:3], in1=st[:, 2:3], op=ALU.mult), [r_st], [r_st])
                V(lambda e: e.scalar_tensor_tensor(out=st[:, 4:5], in0=st[:, 1:2], scalar=1.0 / 1024, in1=st[:, 3:4], op0=ALU.mult, op1=ALU.subtract), [r_st], [r_st])
                A(lambda e: e.activation(out=st[:, 5:6], in_=st[:, 4:5], func=AF.Ln, bias=eps_t[:, 1:2], scale=1.0), [r_st, r_const], [r_st])
                A(lambda e: e.activation(out=st[:, 6:7], in_=st[:, 5:6], func=AF.Exp, scale=-0.5), [r_st], [r_st])
                V(lambda e: e.scalar_tensor_tensor(out=st[:, 7:8], in0=st[:, 2:3], scalar=-1.0, in1=st[:, 6:7], op0=ALU.mult, op1=ALU.mult), [r_st], [r_st])
                V(lambda e: e.tensor_scalar(out=vg[:, sl, :], in0=vg[:, sl, :], scalar1=st[:, 6:7], scalar2=st[:, 7:8], op0=ALU.mult, op1=ALU.add),
                  [r_vg[s], r_st], [r_vg[s]])

            def sgu_mm(s):
                sl = s - a0
                for g in range(8):
                    bank = 6 + g // 4
                    P.emit(pe, lambda e: e.matmul(psum[bank][:, (g % 4) * 128:(g % 4 + 1) * 128],
                                                  vg[:, sl, g * 128:(g + 1) * 128], wtm[:, g, :], start=True, stop=True),
                           reads=[r_vg[s], r_wtm], writes=[r_ps[bank]], inc=True)

            def sgu_post(s):
                t0 = s * 128
                stmp = wk[:, 4:6, :].rearrange("p a b -> p (a b)").rearrange("p (g t) -> p g t", t=128)
                for g in range(8):
                    bank = 6 + g // 4
                    V(lambda e: e.scalar_tensor_tensor(out=stmp[:, g, :], in0=psum[bank][:, (g % 4) * 128:(g % 4 + 1) * 128],
                                                       scalar=par[:, l, 6, g:g + 1], in1=bt[:, g, :], op0=ALU.mult, op1=ALU.add),
                      [r_ps[bank], r_par, r_bt], [r_wk[4], r_wk[5]])
                V(lambda e: e.tensor_tensor(out=uT[:, :, t0:t0 + 128], in0=uT[:, :, t0:t0 + 128], in1=stmp, op=ALU.mult),
                  [r_wk[4], r_wk[5]] + [r_u[(c, s)] for c in range(KC)], [r_u[(c, s)] for c in range(KC)])

            pend = None
            slots = list(range(f0, a1))
            if slots:
                sgu_stats(slots[0])
            for si, s in enumerate(slots):
                for g in range(2):
                    for kb in range(2):
                        pt = att_scores(s, g, kb)
                        if pend is not None:
                            att_pv(*pend)
                        pend = (s, g, kb, pt)
                sgu_mm(s)
                if si + 1 < len(slots):
                    sgu_stats(slots[si + 1])
                sgu_post(s)
            if pend is not None:
                att_pv(*pend)
            bi = pstate["bi"]
            a_y.close()
            GOFF = 3328
            r_g = {}
            for pc in range(8):
                rt, rr = piece([(slotv(256, KC), wview(win, GOFF + pc * 256, 256))])
                wv = rt[:, 0:KC * 256].rearrange("p (k c) -> p k c", c=256)
                for cc in range(2):
                    c = pc * 2 + cc
                    for ti, (s0, ns) in enumerate(tlf):
                        n, t0 = ns * 128, s0 * 128
                        bank = bi % 4; bi += 1
                        mm_group(bank, psum[bank][:, 0:n], [(wv[:, k, cc * 128:(cc + 1) * 128], hT[:, k, t0:t0 + n], hr(k, s0, ns)) for k in range(KC)], [rr])
                        rg = a_y.new()
                        r_g[(c, ti)] = rg
                        A(lambda e, c=c, bank=bank, t0=t0, n=n: e.activation(out=gates[:, c, t0:t0 + n], in_=psum[bank][:, 0:n], func=AF.Sigmoid),
                          [r_ps[bank]], [rg])
            for m in range(KC):
                rt, rr = piece(
                    [(lambda rt_: rt_[:, 0:KC * 128].rearrange("p (k c) -> p k c", c=128), wview(wa_d[l], m * 128, 128)),
                     (lambda rt_: rt_[:, KC * 128:2 * KC * 128].rearrange("p (k c) -> p k c", c=128), wview(ws_d[l], m * 128, 128))])
                wv = rt[:, 0:2 * KC * 128].rearrange("p (k c) -> p k c", c=128)
                for ti, (s0, ns) in enumerate(tlf):
                    n, t0 = ns * 128, s0 * 128
                    par2 = (m * len(tlf) + ti) % 2
                    ba, bb = par2, 2 + par2
                    mm_group(ba, psum[ba][:, 0:n], [(wv[:, k, :], qT[:, k, t0:t0 + n], [r_q[(k // 4, s)] for s in range(s0, s0 + ns)]) for k in range(KC)], [rr])
                    mm_group(bb, psum[bb][:, 0:n], [(wv[:, KC + k, :], uT[:, k, t0:t0 + n], [r_u[(k, s)] for s in range(s0, s0 + ns)]) for k in range(KC)], [rr])
                    wa_, wb_ = par2, 2 + par2
                    V(lambda e, m=m, ba=ba, wa_=wa_, t0=t0, n=n: e.tensor_tensor(out=wk[:, wa_, 0:n], in0=gates[:, m, t0:t0 + n], in1=psum[ba][:, 0:n], op=ALU.mult),
                      [r_g[(m, ti)], r_ps[ba]], [r_wk[wa_]])
                    V(lambda e, m=m, bb=bb, wb_=wb_, t0=t0, n=n: e.tensor_tensor(out=wk[:, wb_, 0:n], in0=gates[:, KC + m, t0:t0 + n], in1=psum[bb][:, 0:n], op=ALU.mult),
                      [r_g[(KC + m, ti)], r_ps[bb]], [r_wk[wb_]])
                    V(lambda e, m=m, wa_=wa_, wb_=wb_, t0=t0, n=n: e.tensor_tensor(out=hT[:, m, t0:t0 + n], in0=wk[:, wa_, 0:n], in1=wk[:, wb_, 0:n], op=ALU.add),
                      [r_wk[wa_], r_wk[wb_]], hr(m, s0, ns))
            a_y.close()
            a_big.close()

            def mk_piece(pc, live):
                return piece([(slotv(256, KC), wview(wo_d[l], pc * 256, 256))], live=live)

            out_and_postnorm(l, 1, tlf, KC, lambda k, t0, n: hT[:, k, t0:t0 + n], lambda k, s0, ns: hr(k, s0, ns), mk_piece, after_tile, mp=2)

        def program():
            state["use"] = 0
            state["issued"] = 0
            cnt["sq"] = 0
            cw = 128 + 3 * 128 + 128
            P.emit(pool, lambda e: e.dma_start(out=ident_bf[:, :], in_=cst_d[:, 0:128]), writes=[r_const], dma_sem=s_c)
            P.emit(pool, lambda e: e.dma_start(out=masks_bf[:, :, :].rearrange("p a b -> p (a b)"), in_=cst_d[:, 128:128 + 384]), stamp_only=[r_const], dma_sem=s_c)
            P.emit(pool, lambda e: e.dma_start(out=tri_bf[:, :], in_=cst_d[:, 128 + 384:cw]), stamp_only=[r_const], dma_sem=s_c)
            P.emit(sp, lambda e: e.dma_start(out=par[:, :, :, :].rearrange("p a b c -> p (a b c)"), in_=par_d[:, :]), writes=[r_par], dma_sem=s_par)
            V(lambda e: e.memset(ones_bf[:, :], 1.0), [], [r_const])
            V(lambda e: e.memset(eps_t[:, 0:1], RMS_EPS), [], [r_const])
            V(lambda e: e.memset(eps_t[:, 1:2], LN_EPS), [], [r_const])
            for l in range(NL):
                for j, (src, c) in enumerate(((1, 0.5), (3, 1.0), (5, 0.5))):
                    V(lambda e, l=l, j=j, src=src, c=c: e.tensor_scalar(out=parh[:, l, j, :], in0=par[:, l, src, :], scalar1=c, scalar2=None, op0=ALU.mult),
                      [r_par], [r_parh])
            for gidx, (g0, g1) in enumerate(GROUPS):
                ng = g1 - g0
                for k in range(KC):
                    P.emit(sp, lambda e, k=k: e.dma_start(out=xT[:, k, 0:ng * 128], in_=xT_d[k * 128:(k + 1) * 128, g0 * 128:g1 * 128]),
                           writes=xr(k, 0, ng), dma_sem=s_x)
                if not P.dry:
                    for k in range(KC):
                        for rg in xr(k, 0, ng):
                            rg.w = (s_x, s_x.n)
                phases = []
                for l in range(NL):
                    a0 = max(0, l - g0) if gidx == 0 else 0
                    f0 = min(a0 + 1, ng) if gidx == 0 else 0
                    if a0 >= ng:
                        continue
                    phases.append(("ffn", l, 0, a0, a0, 0))
                    if f0 < ng:
                        phases.append(("mix", l, 0, a0, f0, 2))
                        phases.append(("ffn", l, 1, f0, f0, 4))
                for pi, (kind, l, which, a0, f0, gpar) in enumerate(phases):
                    tl = tiles_of(a0, ng)
                    tlf = tiles_of(f0, ng)
                    if pi == 0:
                        prenorm(l, gpar, tl)
                    nxt = phases[pi + 1] if pi + 1 < len(phases) else None
                    if nxt is not None:
                        ntl = tiles_of(nxt[3], ng)

                        def after_tile(s0, ns, nxt=nxt, ntl=ntl):
                            assert (s0, ns) in ntl, ((s0, ns), ntl)
                            regs = []
                            for k in range(KC):
                                regs += hr(k, s0, ns)

                            def stage2():
                                prenorm_tile_a(s0, ns)
                                P.defer(lambda: prenorm_tile_b(nxt[1], nxt[5], s0, ns), regs, 16)
                            P.defer(stage2, regs, 24)
                    else:
                        after_tile = None
                    if kind == "ffn":
                        ffn(l, which, tl, after_tile)
                    else:
                        mixer(l, gidx, a0, f0, ng, tl, tlf, HALO - g0, after_tile)
                o0 = max(g0, HALO)
                if o0 < g1:
                    for k in range(KC):
                        P.emit(sp, lambda e, k=k: e.dma_start(out=out_d[k * 128:(k + 1) * 128, (o0 - HALO) * 128:(g1 - HALO) * 128],
                                                              in_=xT[:, k, (o0 - g0) * 128:(g1 - g0) * 128]),
                               reads=xr(k, o0 - g0, g1 - o0), dma_sem=s_o)
                    if not P.dry:
                        for k in range(KC):
                            for rg in xr(k, o0 - g0, g1 - o0):
                                rg.r[id(s_o)] = (s_o, s_o.n)
            P.flush_all()
            if not P.dry:
                tot = s_o.n
                sp.q.append(lambda e: e.wait_ge(s_o.h, tot))

        P.dry = True
        program()
        P.dry = False
        a_big.__init__()
        a_y.__init__()
        program()

        with nc.Block() as block:
            @block.tensor
            def _(e):
                for f in pe.q:
                    f(e)

            @block.scalar
            def _(e):
                for f in act.q:
                    f(e)

            @block.vector
            def _(e):
                for f in dve.q:
                    f(e)

            @block.gpsimd
            def _(e):
                for f in pool.q:
                    f(e)

            @block.sync
            def _(e):
                for f in sp.q:
                    f(e)
    return nc


def make_consts(first_core):
    i = np.arange(128)
    key = i[:, None]
    q = i[None, :]
    ident = np.eye(128, dtype=np.float32)
    mc = np.where(key <= q, 0.0, NEG).astype(np.float32)
    mp = np.where(key > q, 0.0, NEG).astype(np.float32)
    mpf = np.full((128, 128), NEG, np.float32) if first_core else mp
    tri = (key <= q).astype(np.float32)
    return np.ascontiguousarray(np.concatenate([ident, mc, mp, mpf, tri], axis=1))


def pack_common(inp, l0, l1):
    NL = l1 - l0
    sl = slice(l0, l1)

    def fm(g):
        return np.asarray(g[sl], np.float32).reshape(NL, 8, 128).transpose(2, 0, 1)

    names = ["ffn1_pre_g", "ffn1_post_g", "mix_pre_g", "mix_post_g", "ffn2_pre_g", "ffn2_post_g", "sgu_ln_g", "sgu_ln_b"]
    params = np.stack([fm(inp[n]) for n in names], axis=2)
    m = {
        "params": np.ascontiguousarray(params.reshape(128, NL * 64)),
        "sinks_r": np.ascontiguousarray(np.asarray(inp["attn_sinks"][sl], np.float32).reshape(NL, 2, 4, 2).transpose(0, 3, 1, 2).reshape(NL, 2, 8)),
        "sgu_wT": np.ascontiguousarray(np.asarray(inp["sgu_w"][sl], np.float32).transpose(0, 3, 1, 2).reshape(NL, 128, 1024)),
        "sgu_b": np.ascontiguousarray(np.asarray(inp["sgu_b"][sl], np.float32).reshape(NL, 1024)),
    }
    for n in ["ffn1_w1", "ffn2_w1", "ffn1_w2", "ffn2_w2", "w_in", "w_attn_branch", "w_sgu_branch", "w_out"]:
        m[n] = np.ascontiguousarray(np.asarray(inp[n][sl], np.float32))
    return m


def run_stack(x2d, inp, l0, l1, ncores, own, groups_fn, trace=False):
    NL = l1 - l0
    halo = NL
    cfg = {"layers": NL, "halo": halo, "own": own, "groups": groups_fn(halo + own)}
    nc = build(cfg)
    common = pack_common(inp, l0, l1)
    S = x2d.shape[0]
    assert S == ncores * own * 128
    xpad = np.concatenate([np.zeros((halo * 128, D), np.float32), x2d], axis=0)
    in_maps = []
    for c in range(ncores):
        seg = xpad[c * own * 128:(c * own + halo + own) * 128]
        m = dict(common)
        m["xT"] = np.ascontiguousarray(seg.T)
        m["consts"] = make_consts(c == 0)
        in_maps.append(m)
    res = run_bass_kernel_spmd(nc, in_maps, core_ids=list(range(ncores)), **({"trace": True} if trace else {}))
    out = np.concatenate([r["outT"].T for r in res.results], axis=0)
    return np.ascontiguousarray(out), res


def default_groups(ns):
    h = (ns + 1) // 2
    return [(0, h), (h, ns)]


def kernel(**inputs):
    x = np.asarray(inputs["x"], np.float32)
    B, S, _ = x.shape
    out, _ = run_stack(x[0], inputs, 0, 4, 8, 16, default_groups)
    return out.reshape(B, S, D).astype(np.float32)
```

```python
import contextlib
import numpy as np
import concourse.bass as bass
import concourse.mybir as mybir
from concourse.bass_utils import run_bass_kernel_spmd

F32 = mybir.dt.float32
BF16 = mybir.dt.bfloat16
AF = mybir.ActivationFunctionType
ALU = mybir.AluOpType

D = 1024
KC = 8
DFF = 2816
JC = 22
INW = 5376
NEG = -30000.0
RMS_EPS = 1e-6
LN_EPS = 1e-5
SAME_ENG_SYNC = True
RING = 4
SLOT_ELEMS = JC * 128


class Sem:
    def __init__(self, h):
        self.h = h
        self.n = 0


class Eng:
    def __init__(self, name, sem, in_order=False):
        self.name = name
        self.sem = sem
        self.q = []
        self.seen = {}
        self.in_order = in_order


class Reg:
    __slots__ = ("w", "r", "deps")

    def __init__(self, deps=None):
        self.w = None
        self.r = {}
        self.deps = dict(deps) if deps else {}


class Area:
    prog = None
    dirty = False

    def __init__(self):
        self.regs = []
        self.deps = {}

    def new(self):
        if self.dirty:
            if self.prog is not None:
                self.prog.flush_all()
            d = dict(self.deps)
            for r in self.regs:
                for st in ([r.w] if r.w else []) + list(r.r.values()):
                    k = id(st[0])
                    if k not in d or d[k][1] < st[1]:
                        d[k] = st
            self.deps = d
            self.regs = []
            self.dirty = False
        r = Reg(self.deps)
        self.regs.append(r)
        return r

    def close(self):
        self.dirty = True


class _Rec:
    def __getattr__(self, name):
        def f(*a, **k):
            self.call = (name, a, k)
            return self
        return f


class Prog:
    def __init__(self):
        self.dry = False
        self.engs = {}
        self.deferred = []
        self.pending = {}

    def defer(self, fn, regs, after):
        if self.dry:
            fn()
            return
        item = [after, fn, set(id(r) for r in regs)]
        self.deferred.append(item)
        for k in item[2]:
            self.pending[k] = item

    def _run(self, item):
        self.deferred.remove(item)
        for k in item[2]:
            if self.pending.get(k) is item:
                del self.pending[k]
        item[1]()

    def _run_upto(self, item):
        self._run(item)

    def tick(self, n):
        if self.dry:
            return
        for it in self.deferred:
            it[0] -= n
        while True:
            ready = [it for it in self.deferred if it[0] <= 0]
            if not ready:
                break
            self._run(ready[0])

    def flush_all(self):
        while self.deferred:
            self._run(self.deferred[0])

    def emit(self, eng, fn, reads=(), writes=(), inc=True, dma_sem=None, stamp_only=()):
        if self.dry:
            return None
        while self.pending:
            hit = None
            for rg in list(reads) + list(writes):
                hit = self.pending.get(id(rg))
                if hit is not None:
                    break
            if hit is None:
                break
            self._run(hit)
        deps = {}

        def add(st):
            if st is None:
                return
            k = id(st[0])
            if k not in deps or deps[k][1] < st[1]:
                deps[k] = st

        for rg in reads:
            add(rg.w)
            for st in rg.deps.values():
                add(st)
        for rg in writes:
            add(rg.w)
            for st in rg.r.values():
                add(st)
            for st in rg.deps.values():
                add(st)
        waits = []
        for k, (s, v) in deps.items():
            if s is eng.sem and (eng.in_order or not SAME_ENG_SYNC):
                continue
            if eng.seen.get(k, 0) < v:
                eng.seen[k] = v
                waits.append((s.h, v))
        if dma_sem is not None:
            dma_sem.n += 16
            stamp = (dma_sem, dma_sem.n)
            kind = 2
        elif inc:
            eng.sem.n += 1
            stamp = (eng.sem, eng.sem.n)
            kind = 1
        else:
            stamp = (eng.sem, eng.sem.n + 1)
            kind = 0
        semh = stamp[0].h
        rec = _Rec()
        fn(rec)
        cname, cargs, ckw = rec.call

        def run(e):
            for (h, v) in waits:
                e.wait_ge(h, v)
            ins = getattr(e, cname)(*cargs, **ckw)
            if kind == 2:
                ins.then_inc(semh, 16)
            elif kind == 1:
                ins.then_inc(semh, 1)

        eng.q.append(run)
        for rg in reads:
            k = id(stamp[0])
            if k not in rg.r or rg.r[k][1] < stamp[1]:
                rg.r[k] = stamp
        for rg in list(writes) + list(stamp_only):
            rg.w = stamp
            rg.r = {}
            rg.deps = {}
        return stamp


def tiles_of(a0, a1):
    out = []
    e = a1
    while e > a0:
        n = min(4, e - a0)
        out.append((e - n, n))
        e -= n
    return out[::-1]


def build(cfg):
    NL = cfg["layers"]
    HALO = cfg["halo"]
    OWN = cfg["own"]
    NS = HALO + OWN
    GROUPS = cfg["groups"]
    GS = max(b - a for a, b in GROUPS)
    T = GS * 128

    nc = bass.Bass("TRN2", target_bir_lowering=False)
    dt = nc.dram_tensor
    xT_d = dt("xT", [D, NS * 128], F32, kind="ExternalInput").ap()
    w1_d = [dt("ffn1_w1", [NL, D, 2 * DFF], F32, kind="ExternalInput").ap(),
            dt("ffn2_w1", [NL, D, 2 * DFF], F32, kind="ExternalInput").ap()]
    w2_d = [dt("ffn1_w2", [NL, DFF, D], F32, kind="ExternalInput").ap(),
            dt("ffn2_w2", [NL, DFF, D], F32, kind="ExternalInput").ap()]
    win_d = dt("w_in", [NL, D, INW], F32, kind="ExternalInput").ap()
    wa_d = dt("w_attn_branch", [NL, D, D], F32, kind="ExternalInput").ap()
    ws_d = dt("w_sgu_branch", [NL, D, D], F32, kind="ExternalInput").ap()
    wo_d = dt("w_out", [NL, D, D], F32, kind="ExternalInput").ap()
    par_d = dt("params", [128, NL * 8 * 8], F32, kind="ExternalInput").ap()
    sink_d = dt("sinks_r", [NL, 2, 8], F32, kind="ExternalInput").ap()
    sguw_d = dt("sgu_wT", [NL, 128, 1024], F32, kind="ExternalInput").ap()
    sgub_d = dt("sgu_b", [NL, 1024], F32, kind="ExternalInput").ap()
    cst_d = dt("consts", [128, 128 + 3 * 128 + 128], F32, kind="ExternalInput").ap()
    out_d = dt("outT", [D, OWN * 128], F32, kind="ExternalOutput").ap()

    es = contextlib.ExitStack()
    P = Prog()

    def sb(name, shape, dtype):
        return es.enter_context(nc.sbuf_tensor(name, shape, dtype))

    def sem(name):
        return Sem(es.enter_context(nc.semaphore(name)))

    with es:
        xT = sb("xT_sb", [128, KC, T], F32)
        hT = sb("hT_sb", [128, KC, T], BF16)
        big = sb("big_sb", [128, JC * T], BF16)
        ybuf = sb("y_sb", [128, KC * T], F32)
        ring = [sb(f"ring{i}", [128, SLOT_ELEMS], BF16) for i in range(RING)]
        ones_bf = sb("ones_bf", [128, 128], BF16)
        ident_bf = sb("ident_bf", [128, 128], BF16)
        masks_bf = sb("masks_bf", [128, 3, 128], BF16)
        tri_bf = sb("tri_bf", [128, 128], BF16)
        eps_t = sb("eps_t", [128, 2], F32)
        par = sb("par_sb", [128, NL, 8, 8], F32)
        parh = sb("parh_sb", [128, NL, 3, 8], F32)
        wtm = sb("wtm_sb", [128, 8, 128], BF16)
        bt = sb("bt_sb", [128, 8, 128], F32)
        esr = sb("esr_sb", [128, 8], F32)
        ese = sb("ese_sb", [128, 8], F32)
        bk = sb("bk_sb", [128, NL, 2, 128], BF16)
        bv = sb("bv_sb", [128, NL, 128], BF16)
        wk = sb("wk_sb", [128, 6, 512], F32)
        hk = sb("hk_sb", [128, 6, 512], BF16)
        st = sb("st_sb", [128, 16], F32)
        psum = [es.enter_context(nc.psum_tensor(f"ps{i}", [128, 512], F32)) for i in range(8)]

        pe = Eng("pe", sem("s_pe"), in_order=True)
        act = Eng("act", sem("s_act"))
        dve = Eng("dve", sem("s_dve"))
        pool = Eng("pool", sem("s_pool"))
        sp = Eng("sp", sem("s_sp"))
        s_ring = [sem(f"s_ring{i}") for i in range(RING)]
        s_x = sem("s_x")
        s_o = sem("s_o")
        s_c = sem("s_c")
        s_par = sem("s_par")
        s_wraw = sem("s_wraw")
        s_bt = sem("s_bt")
        s_esr = sem("s_esr")

        r_ps = [Reg() for _ in range(8)]
        r_ring = [Reg() for _ in range(RING)]
        r_wk = [Reg() for _ in range(6)]
        r_hk = [Reg() for _ in range(6)]
        r_const = Reg()
        r_par = Reg()
        r_parh = Reg()
        r_x = {}
        r_h = {}
        for k in range(KC):
            for s_ in range(GS):
                r_x[(k, s_)] = Reg()
                r_h[(k, s_)] = Reg()
        a_big = Area()
        a_y = Area()
        a_big.prog = None
        a_y.prog = P
        r_wraw, r_wtm, r_bt, r_esr, r_ese = Reg(), Reg(), Reg(), Reg(), Reg()
        r_bk = [Reg() for _ in range(NL)]
        r_st = Reg()

        def xr(k, s0, n):
            return [r_x[(k, s)] for s in range(s0, s0 + n)]

        def hr(k, s0, n):
            return [r_h[(k, s)] for s in range(s0, s0 + n)]

        pieces = []
        state = {"use": 0, "issued": 0}

        def piece(subs, hold_prev=False, live=1):
            idx = state["use"]
            state["use"] += 1
            if P.dry:
                pieces.append(subs)
                return ring[idx % RING], r_ring[idx % RING]
            while state["issued"] < min(len(pieces), idx - (live - 1) + RING):
                i = state["issued"]
                sl = i % RING
                for si, (dst_fn, src) in enumerate(pieces[i]):
                    dst = dst_fn(ring[sl])
                    P.emit(pool, lambda e, dst=dst, src=src: e.dma_start(out=dst, in_=src),
                           writes=[r_ring[sl]] if si == 0 else (), stamp_only=() if si == 0 else [r_ring[sl]],
                           dma_sem=s_ring[sl])
                state["issued"] += 1
            return ring[idx % RING], r_ring[idx % RING]

        def wview(w2d, c0, ncols):
            return w2d.rearrange("(k p) c -> p k c", p=128)[:, :, c0:c0 + ncols]

        def slotv(width, nk, c0=0, nc_=None):
            nc_ = width if nc_ is None else nc_
            return lambda rt: rt[:, 0:nk * width].rearrange("p (k c) -> p k c", c=width)[:, :, c0:c0 + nc_]

        def mm_group(bank, out_ap, terms, extra_reads=()):
            n = len(terms)
            allregs = list(extra_reads)
            for t_ in terms:
                allregs += t_[2]
            for i, (l_, r_, tregs) in enumerate(terms):
                P.emit(pe, lambda e, l_=l_, r_=r_, i=i: e.matmul(out_ap, l_, r_, start=(i == 0), stop=(i == n - 1)),
                       reads=(list(extra_reads) + list(tregs)) if i == 0 else tregs, writes=[r_ps[bank]], inc=(i == n - 1))
            fin = (pe.sem, pe.sem.n)
            if not P.dry:
                for rg in allregs:
                    rg.r[id(pe.sem)] = fin
            P.tick(n)

        def A(fn, reads, writes):
            P.emit(act, fn, reads, writes)

        def V(fn, reads, writes):
            P.emit(dve, fn, reads, writes)

        def ss_to_rstd(bank, n, eps_col, wtmp, wout):
            A(lambda e: e.activation(out=wk[:, wtmp, 0:n], in_=psum[bank][:, 0:n], func=AF.Ln,
                                     bias=eps_t[:, eps_col:eps_col + 1], scale=1.0 / D),
              [r_ps[bank], r_const], [r_wk[wtmp]])
            A(lambda e: e.activation(out=wk[:, wout, 0:n], in_=wk[:, wtmp, 0:n], func=AF.Exp, scale=-0.5),
              [r_wk[wtmp]], [r_wk[wout]])

        cnt = {"sq": 0}

        def prenorm_tile_a(s0, ns):
            n = ns * 128
            t0 = s0 * 128
            for k in range(KC):
                A(lambda e: e.activation(out=hT[:, k, t0:t0 + n], in_=xT[:, k, t0:t0 + n], func=AF.Square),
                  xr(k, s0, ns), hr(k, s0, ns))

        def prenorm_tile_b(l, gi_par, s0, ns):
            n = ns * 128
            t0 = s0 * 128
            for k in range(KC):
                P.emit(pe, lambda e: e.matmul(psum[7][:, 0:n], ones_bf[:, :], hT[:, k, t0:t0 + n],
                                              start=(k == 0), stop=(k == KC - 1)),
                       reads=hr(k, s0, ns) + [r_const], writes=[r_ps[7]], inc=True)
            ss_to_rstd(7, n, 0, 2, 3)
            for k in range(KC):
                V(lambda e: e.scalar_tensor_tensor(out=hT[:, k, t0:t0 + n], in0=xT[:, k, t0:t0 + n],
                                                   scalar=par[:, l, gi_par, k:k + 1], in1=wk[:, 3, 0:n],
                                                   op0=ALU.mult, op1=ALU.mult),
                  xr(k, s0, ns) + [r_wk[3], r_par], hr(k, s0, ns))

        def prenorm_tile(l, gi_par, s0, ns):
            prenorm_tile_a(s0, ns)
            prenorm_tile_b(l, gi_par, s0, ns)

        def prenorm(l, gi_par, tl):
            for (s0, ns) in tl:
                prenorm_tile(l, gi_par, s0, ns)

        def out_and_postnorm(l, gi_parh, tl, nk, src_fn, src_regs_fn, mk_piece, after_tile=None, mp=1):
            yv = ybuf[:, :].rearrange("p (k t) -> p k t", t=T)
            yreg = {}
            pend_ss = []
            npc = KC // mp
            ntail = min(RING - 1, npc)
            order = []
            for pc in range(npc - ntail):
                for mi_ in range(mp):
                    for ti in range(len(tl)):
                        order.append((pc, pc * mp + mi_, ti))
            for ti in range(len(tl)):
                for pc in range(npc - ntail, npc):
                    for mi_ in range(mp):
                        order.append((pc, pc * mp + mi_, ti))
            loaded = {}
            gcount = 0
            for (pc, m, ti) in order:
                if pc not in loaded:
                    if pc >= npc - ntail:
                        for p2 in range(npc - ntail, npc):
                            loaded[p2] = mk_piece(p2, p2 - (npc - ntail) + 1)
                    else:
                        loaded[pc] = mk_piece(pc, 1)
                rt, rr = loaded[pc]
                wv = rt[:, 0:nk * mp * 128].rearrange("p (k c) -> p k c", c=mp * 128)
                mo = (m % mp) * 128
                s0, ns = tl[ti]
                n = ns * 128
                t0 = s0 * 128
                bank = gcount % 2
                gcount += 1
                last = (m == KC - 1)
                first = (m == 0)
                mm_group(bank, psum[bank][:, 0:n],
                         [(wv[:, k, mo:mo + 128], src_fn(k, t0, n), src_regs_fn(k, s0, ns)) for k in range(nk)],
                         extra_reads=[rr])
                yr = a_y.new()
                yreg[(m, ti)] = yr
                hb = cnt["sq"] % 2
                cnt["sq"] += 1
                A(lambda e: e.activation(out=hk[:, hb, 0:n], in_=psum[bank][:, 0:n], func=AF.Square),
                  [r_ps[bank]], [r_hk[hb]])
                A(lambda e: e.activation(out=yv[:, m, t0:t0 + n], in_=psum[bank][:, 0:n], func=AF.Copy,
                                         scale=parh[:, l, gi_parh, m:m + 1]),
                  [r_ps[bank], r_parh], [yr])

                def ss_mm(hb=hb, ti=ti, n=n, first=first, last=last):
                    P.emit(pe, lambda e: e.matmul(psum[4 + ti][:, 0:n], ones_bf[:, :], hk[:, hb, 0:n],
                                                  start=first, stop=last),
                           reads=[r_hk[hb], r_const], writes=[r_ps[4 + ti]], inc=True)
                if pend_ss:
                    pend_ss.pop()()
                if not last:
                    pend_ss.append(ss_mm)
                else:
                    def chain_a(ss_mm=ss_mm, ti=ti, s0=s0, ns=ns, n=n, t0=t0):
                        ss_mm()
                        ss_to_rstd(4 + ti, n, 0, 2, 3)
                        def dmul(mm_):
                            wb = 4 + (mm_ % 2)
                            V(lambda e: e.tensor_tensor(out=wk[:, wb, 0:n], in0=yv[:, mm_, t0:t0 + n], in1=wk[:, 3, 0:n], op=ALU.mult),
                              [yreg[(mm_, ti)], r_wk[3]], [r_wk[wb]])

                        def dadd(mm_):
                            wb = 4 + (mm_ % 2)
                            V(lambda e: e.tensor_tensor(out=xT[:, mm_, t0:t0 + n], in0=xT[:, mm_, t0:t0 + n],
                                                        in1=wk[:, wb, 0:n], op=ALU.add),
                              [r_wk[wb]] + xr(mm_, s0, ns), xr(mm_, s0, ns))
                        dmul(0)
                        for mm_ in range(1, KC):
                            dmul(mm_)
                            dadd(mm_ - 1)
                        dadd(KC - 1)
                        if after_tile is not None:
                            after_tile(s0, ns)

                    regs = [r_ps[4 + ti]]
                    for k in range(KC):
                        regs += xr(k, s0, ns) + hr(k, s0, ns)
                    P.defer(chain_a, regs, 8)
            assert not pend_ss
            a_y.close()

        def ffn(l, which, tl, after_tile=None):
            w1 = w1_d[which][l]
            w2 = w2_d[which][l]
            hid = big[:, :].rearrange("p (j t) -> p j t", t=T)
            hreg = {}
            nhead = (RING - 1) if len(tl) > 1 else 0
            order = [(j, ti) for ti in range(len(tl)) for j in range(nhead)]
            order += [(j, ti) for j in range(nhead, JC) for ti in range(len(tl))]
            loaded = {}

            def w1_piece(j, live):
                return piece([(slotv(256, KC, 0, 128), wview(w1, j * 128, 128)),
                              (slotv(256, KC, 128, 128), wview(w1, DFF + j * 128, 128))], live=live)

            gcount = 0
            for (j, ti) in order:
                if j not in loaded:
                    if j < nhead:
                        for j2 in range(nhead):
                            loaded[j2] = w1_piece(j2, j2 + 1)
                    else:
                        loaded[j] = w1_piece(j, 1)
                rt, rr = loaded[j]
                wv = rt[:, 0:KC * 256].rearrange("p (k c) -> p k c", c=256)
                s0, ns = tl[ti]
                n = ns * 128
                t0 = s0 * 128
                par2 = gcount % 2
                gcount += 1
                bg, bu = par2, 2 + par2
                mm_group(bg, psum[bg][:, 0:n], [(wv[:, k, 0:128], hT[:, k, t0:t0 + n], hr(k, s0, ns)) for k in range(KC)], [rr])
                mm_group(bu, psum[bu][:, 0:n], [(wv[:, k, 128:256], hT[:, k, t0:t0 + n], hr(k, s0, ns)) for k in range(KC)], [rr])
                A(lambda e: e.activation(out=wk[:, par2, 0:n], in_=psum[bg][:, 0:n], func=AF.Silu),
                  [r_ps[bg]], [r_wk[par2]])
                rg = a_big.new()
                hreg[(j, ti)] = rg
                V(lambda e: e.tensor_tensor(out=hid[:, j, t0:t0 + n], in0=wk[:, par2, 0:n],
                                            in1=psum[bu][:, 0:n], op=ALU.mult),
                  [r_wk[par2], r_ps[bu]], [rg])
            timap = {s0: ti for ti, (s0, ns) in enumerate(tl)}

            def mk_piece(m, live):
                return piece([(slotv(128, JC), wview(w2, m * 128, 128))], live=live)

            out_and_postnorm(l, 0 if which == 0 else 2, tl, JC,
                             lambda j, t0, n: hid[:, j, t0:t0 + n],
                             lambda j, s0, ns: [hreg[(j, timap[s0])]], mk_piece, after_tile)
            a_big.close()

        def mixer(l, gidx, a0, f0, a1, tl, tlf, first_own_local, after_tile=None):
            win = win_d[l]
            o = 0
            qT = big[:, o:o + KC * T].rearrange("p (k t) -> p k t", t=T); o += KC * T
            uT = big[:, o:o + KC * T].rearrange("p (k t) -> p k t", t=T); o += KC * T
            kT = big[:, o:o + 2 * (T + 128)].rearrange("p (g t) -> p g t", t=T + 128); o += 2 * (T + 128)
            vv = big[:, o:o + (GS + 1) * 128].rearrange("p (s c) -> p s c", c=128); o += (GS + 1) * 128
            assert o <= JC * T
            vg = ybuf[:, :].bitcast(BF16)[:, 0:GS * 1024].rearrange("p (s c) -> p s c", c=1024)
            gates = ybuf[:, :].bitcast(BF16)[:, 0:16 * T].rearrange("p (k t) -> p k t", t=T)
            r_q = {(g, s): a_big.new() for g in range(2) for s in range(f0, a1)}
            r_u = {(c, s): a_big.new() for c in range(KC) for s in range(f0, a1)}
            r_k = {s: a_big.new() for s in range(a0 - 1, a1)}
            r_v = {s: a_big.new() for s in range(a0 - 1, a1)}
            P.emit(pool, lambda e: e.dma_start(out=wtm[:, :, :].rearrange("p g t -> p (g t)"), in_=sguw_d[l]), writes=[r_wtm], dma_sem=s_wraw)
            for g in range(8):
                V(lambda e, g=g: e.tensor_tensor(out=wtm[:, g, :], in0=wtm[:, g, :], in1=tri_bf[:, :], op=ALU.mult),
                  [r_wtm, r_const], [r_wtm])
            P.emit(sp, lambda e: e.dma_start(out=bt[:, :, :].rearrange("p g t -> p (g t)"),
                                             in_=sgub_d[l].partition_broadcast(128)), writes=[r_bt], dma_sem=s_bt)
            for e_ in range(2):
                P.emit(sp, lambda e, e_=e_: e.dma_start(out=esr[64 * e_:64 * e_ + 64, :],
                                                        in_=sink_d[l, e_].partition_broadcast(64)),
                       writes=[r_esr] if e_ == 0 else (), stamp_only=() if e_ == 0 else [r_esr], dma_sem=s_esr)
            A(lambda e: e.activation(out=ese[:, :], in_=esr[:, :], func=AF.Exp), [r_esr], [r_ese])
            wtm2 = wtm[:, :, :].rearrange("p g t -> p (g t)")
            for hf in range(2):
                P.emit(pe, lambda e, hf=hf: e.matmul(psum[6 + hf][:, :], ones_bf[:, :], wtm2[:, hf * 512:(hf + 1) * 512],
                                                     start=True, stop=True),
                       reads=[r_wtm, r_const], writes=[r_ps[6 + hf]], inc=True)
                for gg in range(4):
                    g = hf * 4 + gg
                    V(lambda e, g=g, gg=gg, hf=hf: e.scalar_tensor_tensor(out=bt[:, g, :], in0=psum[6 + hf][:, gg * 128:(gg + 1) * 128],
                                                                          scalar=par[:, l, 7, g:g + 1], in1=bt[:, g, :],
                                                                          op0=ALU.mult, op1=ALU.add),
                      [r_ps[6 + hf], r_par, r_bt], [r_bt])
            if gidx == 0:
                V(lambda e: e.memset(kT[:, :, 0:128], 0.0), [], [r_k[a0 - 1]])
                V(lambda e: e.memset(vv[:, 0, :], 0.0), [], [r_v[a0 - 1]])
            else:
                V(lambda e: e.tensor_copy(out=kT[:, :, 0:128], in_=bk[:, l, :, :]), [r_bk[l]], [r_k[a0 - 1]])
                V(lambda e: e.tensor_copy(out=vv[:, 0, :], in_=bv[:, l, :]), [r_bk[l]], [r_v[a0 - 1]])
            KOFF = 1024
            rtk, rrk = piece([(slotv(256, KC, 0, 64), wview(win, KOFF, 64)), (slotv(256, KC, 64, 64), wview(win, KOFF, 64)),
                              (slotv(256, KC, 128, 64), wview(win, KOFF + 64, 64)), (slotv(256, KC, 192, 64), wview(win, KOFF + 64, 64))])
            wvk = rtk[:, 0:KC * 256].rearrange("p (k c) -> p k c", c=256)
            rtv, rrv = piece([(slotv(128, KC), wview(win, KOFF + 128, 128))], live=2)
            wvv = rtv[:, 0:KC * 128].rearrange("p (k c) -> p k c", c=128)
            rtq0, rrq0 = piece([(slotv(256, KC), wview(win, 0, 256))], live=3)
            wvq0 = rtq0[:, 0:KC * 256].rearrange("p (k c) -> p k c", c=256)
            tlf_set = set(tlf)
            bi = 0
            for (s0, ns) in tl:
                n, t0 = ns * 128, s0 * 128
                for g in range(2):
                    bank = bi % 4; bi += 1
                    mm_group(bank, psum[bank][:, 0:n], [(wvk[:, k, g * 128:(g + 1) * 128], hT[:, k, t0:t0 + n], hr(k, s0, ns)) for k in range(KC)], [rrk])
                    lp = (s0 - a0 + 1) * 128
                    A(lambda e: e.activation(out=kT[:, g, lp:lp + n], in_=psum[bank][:, 0:n], func=AF.Copy),
                      [r_ps[bank]], [r_k[s_] for s_ in range(s0, s0 + ns)])
                for s_ in range(s0, s0 + ns):
                    bank = bi % 4; bi += 1
                    ts_ = s_ * 128
                    mm_group(bank, psum[bank][:, 0:128], [(hT[:, k, ts_:ts_ + 128], wvv[:, k, :], hr(k, s_, 1)) for k in range(KC)], [rrv])
                    A(lambda e: e.activation(out=vv[:, s_ - a0 + 1, :], in_=psum[bank][:, 0:128], func=AF.Copy),
                      [r_ps[bank]], [r_v[s_]])
                qs0 = max(s0, f0)
                qns = s0 + ns - qs0
                if qns > 0:
                    assert (qs0, qns) in tlf_set, ((qs0, qns), tlf)
                    qn, qt0 = qns * 128, qs0 * 128
                    for c in range(2):
                        bank = bi % 4; bi += 1
                        mm_group(bank, psum[bank][:, 0:qn], [(wvq0[:, k, c * 128:(c + 1) * 128], hT[:, k, qt0:qt0 + qn], hr(k, qs0, qns)) for k in range(KC)], [rrq0])
                        A(lambda e: e.activation(out=qT[:, c, qt0:qt0 + qn], in_=psum[bank][:, 0:qn], func=AF.Copy),
                          [r_ps[bank]], [r_q[(0, s_)] for s_ in range(qs0, qs0 + qns)])
            if gidx + 1 < len(GROUPS):
                lpl = (a1 - 1 - a0 + 1) * 128
                V(lambda e: e.tensor_copy(out=bk[:, l, :, :], in_=kT[:, :, lpl:lpl + 128]), [r_k[a1 - 1]], [r_bk[l]])
                V(lambda e: e.tensor_copy(out=bv[:, l, :], in_=vv[:, a1 - 1 - a0 + 1, :]), [r_v[a1 - 1]], [r_bk[l]])
            for pc in range(1, 4):
                rt, rr = piece([(slotv(256, KC), wview(win, pc * 256, 256))])
                wv = rt[:, 0:KC * 256].rearrange("p (k c) -> p k c", c=256)
                for cc in range(2):
                    c = pc * 2 + cc
                    for (s0, ns) in tlf:
                        n, t0 = ns * 128, s0 * 128
                        bank = bi % 4; bi += 1
                        mm_group(bank, psum[bank][:, 0:n], [(wv[:, k, cc * 128:(cc + 1) * 128], hT[:, k, t0:t0 + n], hr(k, s0, ns)) for k in range(KC)], [rr])
                        A(lambda e, c=c, bank=bank, t0=t0, n=n: e.activation(out=qT[:, c, t0:t0 + n], in_=psum[bank][:, 0:n], func=AF.Copy),
                          [r_ps[bank]], [r_q[(c // 4, s)] for s in range(s0, s0 + ns)])
            UOFF = 1280
            for pc in range(4):
                rt, rr = piece([(slotv(256, KC), wview(win, UOFF + pc * 256, 256))])
                wv = rt[:, 0:KC * 256].rearrange("p (k c) -> p k c", c=256)
                for cc in range(2):
                    c = pc * 2 + cc
                    for (s0, ns) in tlf:
                        n, t0 = ns * 128, s0 * 128
                        bank = bi % 4; bi += 1
                        mm_group(bank, psum[bank][:, 0:n], [(wv[:, k, cc * 128:(cc + 1) * 128], hT[:, k, t0:t0 + n], hr(k, s0, ns)) for k in range(KC)], [rr])
                        A(lambda e, c=c, bank=bank, t0=t0, n=n: e.activation(out=uT[:, c, t0:t0 + n], in_=psum[bank][:, 0:n], func=AF.Gelu),
                          [r_ps[bank]], [r_u[(c, s)] for s in range(s0, s0 + ns)])
            VOFF = 2304
            r_vg = {s: a_y.new() for s in range(f0, a1)}
            for pc in range(4):
                rt, rr = piece([(slotv(256, KC), wview(win, VOFF + pc * 256, 256))])
                wv = rt[:, 0:KC * 256].rearrange("p (k c) -> p k c", c=256)
                for s in range(f0, a1):
                    t0 = s * 128
                    bank = bi % 4; bi += 1
                    mm_group(bank, psum[bank][:, 0:256], [(hT[:, k, t0:t0 + 128], wv[:, k, :], hr(k, s, 1)) for k in range(KC)], [rr])
                    A(lambda e, bank=bank, s=s, pc=pc: e.activation(out=vg[:, s - a0, pc * 256:(pc + 1) * 256], in_=psum[bank][:, 0:256], func=AF.Gelu),
                      [r_ps[bank]], [r_vg[s]])
            pstate = {"pc": 0, "bi": bi}

            def att_scores(s, g, kb):
                t0 = s * 128
                lp = (s - a0 + 1) * 128
                mi = 0 if kb == 1 else (2 if s == first_own_local else 1)
                kl = lp - 128 if kb == 0 else lp
                ks = s - 1 if kb == 0 else s
                banks = []
                for e_ in range(2):
                    bank = pstate["bi"] % 3; pstate["bi"] += 1
                    banks.append(bank)
                    for cc in range(4):
                        P.emit(pe, lambda e: e.matmul(psum[bank][:, cc * 128:(cc + 1) * 128], ident_bf[:, :], masks_bf[:, mi, :],
                                                      start=(cc == 0), stop=False),
                               reads=[r_const, r_k[ks], r_q[(g, s)]] if cc == 0 else (), writes=[r_ps[bank]], inc=False)
                for cc in range(4):
                    c = 4 * g + cc
                    for e_ in range(2):
                        bank = banks[e_]
                        last = (cc == 3 and e_ == 1)
                        P.emit(pe, lambda e: e.matmul(psum[bank][:, cc * 128:(cc + 1) * 128],
                                                      kT[64 * e_:64 * e_ + 64, g, kl:kl + 128],
                                                      qT[64 * e_:64 * e_ + 64, c, t0:t0 + 128], start=False, stop=(cc == 3)),
                               reads=(), writes=[r_ps[bank]], inc=last)
                if not P.dry:
                    fin = (pe.sem, pe.sem.n)
                    r_k[ks].r[id(pe.sem)] = fin
                    r_q[(g, s)].r[id(pe.sem)] = fin
                ptiles = []
                for e_ in range(2):
                    bank = banks[e_]
                    hb = 2 + (pstate["pc"] % 4); pstate["pc"] += 1
                    A(lambda e: e.activation(out=hk[:, hb, :], in_=psum[bank][:, :], func=AF.Exp, scale=0.125),
                      [r_ps[bank]], [r_hk[hb]])
                    ptiles.append(hb)
                return ptiles

            def att_pv(s, g, kb, ptiles):
                bo, bd = 3 + g, 5
                ks = s - 1 if kb == 0 else s
                lpk = (ks - a0 + 1)
                e_ = kb
                for which_, bnk in ((0, bo), (1, bd)):
                    lhs = vv[:, lpk, 64 * g:64 * g + 64] if which_ == 0 else ones_bf[:, 0:64]
                    for ep in range(2):
                        hb = ptiles[ep]
                        last = (which_ == 1 and ep == 1)
                        P.emit(pe, lambda e: e.matmul(psum[bnk][64 * ep:64 * ep + 64, :], lhs, hk[:, hb, :], start=(kb == 0), stop=(kb == 1)),
                               reads=[r_hk[hb], r_v[ks], r_const], writes=[r_ps[bnk]], inc=last)
                if e_ == 1:
                    t0 = s * 128
                    for cc in range(4):
                        A(lambda e: e.activation(out=wk[:, 0, cc * 128:(cc + 1) * 128], in_=psum[bd][:, cc * 128:(cc + 1) * 128], func=AF.Ln,
                                                 bias=ese[:, 4 * g + cc:4 * g + cc + 1], scale=1.0),
                          [r_ps[bd], r_ese], [r_wk[0]])
                    A(lambda e: e.activation(out=wk[:, 1, :], in_=wk[:, 0, :], func=AF.Exp, scale=-1.0), [r_wk[0]], [r_wk[1]])
                    V(lambda e: e.tensor_tensor(out=qT[:, 4 * g:4 * g + 4, t0:t0 + 128],
                                                in0=psum[bo][:, :].rearrange("p (c t) -> p c t", t=128),
                                                in1=wk[:, 1, :].rearrange("p (c t) -> p c t", t=128), op=ALU.mult),
                      [r_ps[bo], r_wk[1]], [r_q[(g, s)]])

            def sgu_stats(s):
                sl = s - a0
                junk = wk[:, 2:4, :].rearrange("p a b -> p (a b)")
                V(lambda e: e.tensor_scalar(out=junk, in0=vg[:, sl, :], scalar1=1.0, scalar2=0.0, op0=ALU.mult, op1=ALU.add, accum_out=st[:, 0:1]),
                  [r_vg[s]], [r_wk[2], r_wk[3], r_st])
                V(lambda e: e.scalar_tensor_tensor(out=junk, in0=vg[:, sl, :], scalar=1.0, in1=vg[:, sl, :], op0=ALU.mult, op1=ALU.mult, accum_out=st[:, 1:2]),
                  [r_vg[s]], [r_wk[2], r_wk[3], r_st])
                V(lambda e: e.tensor_scalar(out=st[:, 2:3], in0=st[:, 0:1], scalar1=1.0 / 1024, scalar2=None, op0=ALU.mult), [r_st], [r_st])
                V(lambda e: e.tensor_tensor(out=st[:, 3:4], in0=st[:, 2:3], in1=st[:, 2:3], op=ALU.mult), [r_st], [r_st])
                V(lambda e: e.scalar_tensor_tensor(out=st[:, 4:5], in0=st[:, 1:2], scalar=1.0 / 1024, in1=st[:, 3:4], op0=ALU.mult, op1=ALU.subtract), [r_st], [r_st])
                A(lambda e: e.activation(out=st[:, 5:6], in_=st[:, 4:5], func=AF.Ln, bias=eps_t[:, 1:2], scale=1.0), [r_st, r_const], [r_st])
                A(lambda e: e.activation(out=st[:, 6:7], in_=st[:, 5:6], func=AF.Exp, scale=-0.5), [r_st], [r_st])
                V(lambda e: e.scalar_tensor_tensor(out=st[:, 7:8], in0=st[:, 2:3], scalar=-1.0, in1=st[:, 6:7], op0=ALU.mult, op1=ALU.mult), [r_st], [r_st])
                V(lambda e: e.tensor_scalar(out=vg[:, sl, :], in0=vg[:, sl, :], scalar1=st[:, 6:7], scalar2=st[:, 7:8], op0=ALU.mult, op1=ALU.add),
                  [r_vg[s], r_st], [r_vg[s]])

            def sgu_mm(s):
                sl = s - a0
                for g in range(8):
                    bank = 6 + g // 4
                    P.emit(pe, lambda e: e.matmul(psum[bank][:, (g % 4) * 128:(g % 4 + 1) * 128],
                                                  vg[:, sl, g * 128:(g + 1) * 128], wtm[:, g, :], start=True, stop=True),
                           reads=[r_vg[s], r_wtm], writes=[r_ps[bank]], inc=True)

            def sgu_post(s):
                t0 = s * 128
                stmp = wk[:, 4:6, :].rearrange("p a b -> p (a b)").rearrange("p (g t) -> p g t", t=128)
                for g in range(8):
                    bank = 6 + g // 4
                    V(lambda e: e.scalar_tensor_tensor(out=stmp[:, g, :], in0=psum[bank][:, (g % 4) * 128:(g % 4 + 1) * 128],
                                                       scalar=par[:, l, 6, g:g + 1], in1=bt[:, g, :], op0=ALU.mult, op1=ALU.add),
                      [r_ps[bank], r_par, r_bt], [r_wk[4], r_wk[5]])
                V(lambda e: e.tensor_tensor(out=uT[:, :, t0:t0 + 128], in0=uT[:, :, t0:t0 + 128], in1=stmp, op=ALU.mult),
                  [r_wk[4], r_wk[5]] + [r_u[(c, s)] for c in range(KC)], [r_u[(c, s)] for c in range(KC)])

            pend = None
            slots = list(range(f0, a1))
            if slots:
                sgu_stats(slots[0])
            for si, s in enumerate(slots):
                for g in range(2):
                    for kb in range(2):
                        pt = att_scores(s, g, kb)
                        if pend is not None:
                            att_pv(*pend)
                        pend = (s, g, kb, pt)
                sgu_mm(s)
                if si + 1 < len(slots):
                    sgu_stats(slots[si + 1])
                sgu_post(s)
            if pend is not None:
                att_pv(*pend)
            bi = pstate["bi"]
            a_y.close()
            GOFF = 3328
            r_g = {}
            for pc in range(8):
                rt, rr = piece([(slotv(256, KC), wview(win, GOFF + pc * 256, 256))])
                wv = rt[:, 0:KC * 256].rearrange("p (k c) -> p k c", c=256)
                for cc in range(2):
                    c = pc * 2 + cc
                    for ti, (s0, ns) in enumerate(tlf):
                        n, t0 = ns * 128, s0 * 128
                        bank = bi % 4; bi += 1
                        mm_group(bank, psum[bank][:, 0:n], [(wv[:, k, cc * 128:(cc + 1) * 128], hT[:, k, t0:t0 + n], hr(k, s0, ns)) for k in range(KC)], [rr])
                        rg = a_y.new()
                        r_g[(c, ti)] = rg
                        A(lambda e, c=c, bank=bank, t0=t0, n=n: e.activation(out=gates[:, c, t0:t0 + n], in_=psum[bank][:, 0:n], func=AF.Sigmoid),
                          [r_ps[bank]], [rg])
            for m in range(KC):
                rt, rr = piece(
                    [(lambda rt_: rt_[:, 0:KC * 128].rearrange("p (k c) -> p k c", c=128), wview(wa_d[l], m * 128, 128)),
                     (lambda rt_: rt_[:, KC * 128:2 * KC * 128].rearrange("p (k c) -> p k c", c=128), wview(ws_d[l], m * 128, 128))])
                wv = rt[:, 0:2 * KC * 128].rearrange("p (k c) -> p k c", c=128)
                for ti, (s0, ns) in enumerate(tlf):
                    n, t0 = ns * 128, s0 * 128
                    par2 = (m * len(tlf) + ti) % 2
                    ba, bb = par2, 2 + par2
                    mm_group(ba, psum[ba][:, 0:n], [(wv[:, k, :], qT[:, k, t0:t0 + n], [r_q[(k // 4, s)] for s in range(s0, s0 + ns)]) for k in range(KC)], [rr])
                    mm_group(bb, psum[bb][:, 0:n], [(wv[:, KC + k, :], uT[:, k, t0:t0 + n], [r_u[(k, s)] for s in range(s0, s0 + ns)]) for k in range(KC)], [rr])
                    wa_, wb_ = par2, 2 + par2
                    V(lambda e, m=m, ba=ba, wa_=wa_, t0=t0, n=n: e.tensor_tensor(out=wk[:, wa_, 0:n], in0=gates[:, m, t0:t0 + n], in1=psum[ba][:, 0:n], op=ALU.mult),
                      [r_g[(m, ti)], r_ps[ba]], [r_wk[wa_]])
                    V(lambda e, m=m, bb=bb, wb_=wb_, t0=t0, n=n: e.tensor_tensor(out=wk[:, wb_, 0:n], in0=gates[:, KC + m, t0:t0 + n], in1=psum[bb][:, 0:n], op=ALU.mult),
                      [r_g[(KC + m, ti)], r_ps[bb]], [r_wk[wb_]])
                    V(lambda e, m=m, wa_=wa_, wb_=wb_, t0=t0, n=n: e.tensor_tensor(out=hT[:, m, t0:t0 + n], in0=wk[:, wa_, 0:n], in1=wk[:, wb_, 0:n], op=ALU.add),
                      [r_wk[wa_], r_wk[wb_]], hr(m, s0, ns))
            a_y.close()
            a_big.close()

            def mk_piece(pc, live):
                return piece([(slotv(256, KC), wview(wo_d[l], pc * 256, 256))], live=live)

            out_and_postnorm(l, 1, tlf, KC, lambda k, t0, n: hT[:, k, t0:t0 + n], lambda k, s0, ns: hr(k, s0, ns), mk_piece, after_tile, mp=2)

        def program():
            state["use"] = 0
            state["issued"] = 0
            cnt["sq"] = 0
            cw = 128 + 3 * 128 + 128
            P.emit(pool, lambda e: e.dma_start(out=ident_bf[:, :], in_=cst_d[:, 0:128]), writes=[r_const], dma_sem=s_c)
            P.emit(pool, lambda e: e.dma_start(out=masks_bf[:, :, :].rearrange("p a b -> p (a b)"), in_=cst_d[:, 128:128 + 384]), stamp_only=[r_const], dma_sem=s_c)
            P.emit(pool, lambda e: e.dma_start(out=tri_bf[:, :], in_=cst_d[:, 128 + 384:cw]), stamp_only=[r_const], dma_sem=s_c)
            P.emit(sp, lambda e: e.dma_start(out=par[:, :, :, :].rearrange("p a b c -> p (a b c)"), in_=par_d[:, :]), writes=[r_par], dma_sem=s_par)
            V(lambda e: e.memset(ones_bf[:, :], 1.0), [], [r_const])
            V(lambda e: e.memset(eps_t[:, 0:1], RMS_EPS), [], [r_const])
            V(lambda e: e.memset(eps_t[:, 1:2], LN_EPS), [], [r_const])
            for l in range(NL):
                for j, (src, c) in enumerate(((1, 0.5), (3, 1.0), (5, 0.5))):
                    V(lambda e, l=l, j=j, src=src, c=c: e.tensor_scalar(out=parh[:, l, j, :], in0=par[:, l, src, :], scalar1=c, scalar2=None, op0=ALU.mult),
                      [r_par], [r_parh])
            for gidx, (g0, g1) in enumerate(GROUPS):
                ng = g1 - g0
                for k in range(KC):
                    P.emit(sp, lambda e, k=k: e.dma_start(out=xT[:, k, 0:ng * 128], in_=xT_d[k * 128:(k + 1) * 128, g0 * 128:g1 * 128]),
                           writes=xr(k, 0, ng), dma_sem=s_x)
                if not P.dry:
                    for k in range(KC):
                        for rg in xr(k, 0, ng):
                            rg.w = (s_x, s_x.n)
                phases = []
                for l in range(NL):
                    a0 = max(0, l - g0) if gidx == 0 else 0
                    f0 = min(a0 + 1, ng) if gidx == 0 else 0
                    if a0 >= ng:
                        continue
                    phases.append(("ffn", l, 0, a0, a0, 0))
                    if f0 < ng:
                        phases.append(("mix", l, 0, a0, f0, 2))
                        phases.append(("ffn", l, 1, f0, f0, 4))
                for pi, (kind, l, which, a0, f0, gpar) in enumerate(phases):
                    tl = tiles_of(a0, ng)
                    tlf = tiles_of(f0, ng)
                    if pi == 0:
                        prenorm(l, gpar, tl)
                    nxt = phases[pi + 1] if pi + 1 < len(phases) else None
                    if nxt is not None:
                        ntl = tiles_of(nxt[3], ng)

                        def after_tile(s0, ns, nxt=nxt, ntl=ntl):
                            assert (s0, ns) in ntl, ((s0, ns), ntl)
                            regs = []
                            for k in range(KC):
                                regs += hr(k, s0, ns)

                            def stage2():
                                prenorm_tile_a(s0, ns)
                                P.defer(lambda: prenorm_tile_b(nxt[1], nxt[5], s0, ns), regs, 24)
                            P.defer(stage2, regs, 40)
                    else:
                        after_tile = None
                    if kind == "ffn":
                        ffn(l, which, tl, after_tile)
                    else:
                        mixer(l, gidx, a0, f0, ng, tl, tlf, HALO - g0, after_tile)
                o0 = max(g0, HALO)
                if o0 < g1:
                    for k in range(KC):
                        P.emit(sp, lambda e, k=k: e.dma_start(out=out_d[k * 128:(k + 1) * 128, (o0 - HALO) * 128:(g1 - HALO) * 128],
                                                              in_=xT[:, k, (o0 - g0) * 128:(g1 - g0) * 128]),
                               reads=xr(k, o0 - g0, g1 - o0), dma_sem=s_o)
                    if not P.dry:
                        for k in range(KC):
                            for rg in xr(k, o0 - g0, g1 - o0):
                                rg.r[id(s_o)] = (s_o, s_o.n)
            P.flush_all()
            if not P.dry:
                tot = s_o.n
                sp.q.append(lambda e: e.wait_ge(s_o.h, tot))

        P.dry = True
        program()
        P.dry = False
        a_big.__init__()
        a_y.__init__()
        program()

        with nc.Block() as block:
            @block.tensor
            def _(e):
                for f in pe.q:
                    f(e)

            @block.scalar
            def _(e):
                for f in act.q:
                    f(e)

            @block.vector
            def _(e):
                for f in dve.q:
                    f(e)

            @block.gpsimd
            def _(e):
                for f in pool.q:
                    f(e)

            @block.sync
            def _(e):
                for f in sp.q:
                    f(e)
    return nc


def make_consts(first_core):
    i = np.arange(128)
    key = i[:, None]
    q = i[None, :]
    ident = np.eye(128, dtype=np.float32)
    mc = np.where(key <= q, 0.0, NEG).astype(np.float32)
    mp = np.where(key > q, 0.0, NEG).astype(np.float32)
    mpf = np.full((128, 128), NEG, np.float32) if first_core else mp
    tri = (key <= q).astype(np.float32)
    return np.ascontiguousarray(np.concatenate([ident, mc, mp, mpf, tri], axis=1))


def pack_common(inp, l0, l1):
    NL = l1 - l0
    sl = slice(l0, l1)

    def fm(g):
        return np.asarray(g[sl], np.float32).reshape(NL, 8, 128).transpose(2, 0, 1)

    names = ["ffn1_pre_g", "ffn1_post_g", "mix_pre_g", "mix_post_g", "ffn2_pre_g", "ffn2_post_g", "sgu_ln_g", "sgu_ln_b"]
    params = np.stack([fm(inp[n]) for n in names], axis=2)
    m = {
        "params": np.ascontiguousarray(params.reshape(128, NL * 64)),
        "sinks_r": np.ascontiguousarray(np.asarray(inp["attn_sinks"][sl], np.float32).reshape(NL, 2, 4, 2).transpose(0, 3, 1, 2).reshape(NL, 2, 8)),
        "sgu_wT": np.ascontiguousarray(np.asarray(inp["sgu_w"][sl], np.float32).transpose(0, 3, 1, 2).reshape(NL, 128, 1024)),
        "sgu_b": np.ascontiguousarray(np.asarray(inp["sgu_b"][sl], np.float32).reshape(NL, 1024)),
    }
    for n in ["ffn1_w1", "ffn2_w1", "ffn1_w2", "ffn2_w2", "w_in", "w_attn_branch", "w_sgu_branch", "w_out"]:
        m[n] = np.ascontiguousarray(np.asarray(inp[n][sl], np.float32))
    return m


def run_stack(x2d, inp, l0, l1, ncores, own, groups_fn, trace=False):
    NL = l1 - l0
    halo = NL
    cfg = {"layers": NL, "halo": halo, "own": own, "groups": groups_fn(halo + own)}
    nc = build(cfg)
    common = pack_common(inp, l0, l1)
    S = x2d.shape[0]
    assert S == ncores * own * 128
    xpad = np.concatenate([np.zeros((halo * 128, D), np.float32), x2d], axis=0)
    in_maps = []
    for c in range(ncores):
        seg = xpad[c * own * 128:(c * own + halo + own) * 128]
        m = dict(common)
        m["xT"] = np.ascontiguousarray(seg.T)
        m["consts"] = make_consts(c == 0)
        in_maps.append(m)
    res = run_bass_kernel_spmd(nc, in_maps, core_ids=list(range(ncores)), **({"trace": True} if trace else {}))
    out = np.concatenate([r["outT"].T for r in res.results], axis=0)
    return np.ascontiguousarray(out), res


def default_groups(ns):
    h = (ns + 1) // 2
    return [(0, h), (h, ns)]


def kernel(**inputs):
    x = np.asarray(inputs["x"], np.float32)
    B, S, _ = x.shape
    out, _ = run_stack(x[0], inputs, 0, 4, 8, 16, default_groups)
    return out.reshape(B, S, D).astype(np.float32)
```
